# Optimizing a Trainium2 kernel written in Bass

```python
import jax, jax.numpy as jnp
from jax import lax
import numpy as np

D_MODEL = 2048
BATCH = 8
SEQ = 4096
DEPTH = 2

N_META = 16
MIX_WIDTH = D_MODEL
CONV_WIDTH = MIX_WIDTH // 2
ATTN_WIDTH = MIX_WIDTH - CONV_WIDTH
HEAD_DIM = 128
N_ATTN_HEADS = ATTN_WIDTH // HEAD_DIM
CONV_GROUP = 128
N_CONV_GROUPS = CONV_WIDTH // CONV_GROUP
CONV_K = 3
D_FF = 4 * D_MODEL
Q_BLOCK = 128
EPS = 1e-6
IN_PROJ_WIDTH = 3 * CONV_WIDTH + 3 * ATTN_WIDTH

kernel_name = "hybrid_shortconv_stickbreaking_block"


def rms_norm(x, g):
    xf = x.astype(jnp.float32)
    y = xf * lax.rsqrt(jnp.mean(xf * xf, axis=-1, keepdims=True) + EPS)
    return (y * g.astype(jnp.float32)).astype(x.dtype)


def group_rms_norm(x, g, n_groups):
    lead = x.shape[:-1]
    c = x.shape[-1]
    xg = x.reshape(*lead, n_groups, c // n_groups)
    return rms_norm(xg, g.reshape(n_groups, c // n_groups)).reshape(*lead, c)


def causal_short_conv(u, w):
    length = u.shape[1]
    up = jnp.pad(u, ((0, 0), (CONV_K - 1, 0), (0, 0)))
    y = w[0] * up[:, 0:length]
    for tap in range(1, CONV_K):
        y = y + w[tap] * up[:, tap:tap + length]
    return y


def _stick_breaking_attend(q, k, v, q_pos, k_pos):
    z = jnp.einsum('bhqd,bhkd->bhqk', q, k).astype(jnp.float32) * (HEAD_DIM ** -0.5)
    visible = k_pos[None, :] < q_pos[:, None]
    log_1m_beta = jnp.where(visible, jax.nn.log_sigmoid(-z), 0.0)
    suffix = lax.cumsum(log_1m_beta, axis=3, reverse=True) - log_1m_beta
    a = jnp.where(visible, jnp.exp(jax.nn.log_sigmoid(z) + suffix), 0.0)
    return jnp.einsum('bhqk,bhkd->bhqd', a.astype(v.dtype), v)


def stick_breaking_attention(q, k, v):
    b, h, length, dh = q.shape
    meta_pos = jnp.arange(N_META)
    meta_out = _stick_breaking_attend(q[:, :, :N_META], k[:, :, :N_META], v[:, :, :N_META],
                                      meta_pos, meta_pos)
    n_blocks = (length - N_META) // Q_BLOCK
    q_blocks = q[:, :, N_META:].reshape(b, h, n_blocks, Q_BLOCK, dh).transpose(2, 0, 1, 3, 4)
    k_pos = jnp.arange(length)

    def one_block(args):
        q_blk, blk_idx = args
        q_pos = N_META + blk_idx * Q_BLOCK + jnp.arange(Q_BLOCK)
        return _stick_breaking_attend(q_blk, k, v, q_pos, k_pos)

    out = lax.map(one_block, (q_blocks, jnp.arange(n_blocks)))
    out = out.transpose(1, 2, 0, 3, 4).reshape(b, h, length - N_META, dh)
    return jnp.concatenate([meta_out, out], axis=2)


def hybrid_mixer(x, g_mix, w_in, conv_w, g_q, g_k, g_conv_out, g_attn_out, w_out):
    b, length, _ = x.shape
    hn = rms_norm(x, g_mix)
    proj = hn @ w_in
    splits = [CONV_WIDTH, 2 * CONV_WIDTH, 3 * CONV_WIDTH,
              3 * CONV_WIDTH + ATTN_WIDTH, 3 * CONV_WIDTH + 2 * ATTN_WIDTH]
    gate_b, gate_c, u, q, k, v = jnp.split(proj, splits, axis=-1)
    y_conv = gate_b * causal_short_conv(gate_c * u, conv_w)
    y_conv = group_rms_norm(y_conv, g_conv_out, N_CONV_GROUPS)
    q = rms_norm(q.reshape(b, length, N_ATTN_HEADS, HEAD_DIM), g_q).transpose(0, 2, 1, 3)
    k = rms_norm(k.reshape(b, length, N_ATTN_HEADS, HEAD_DIM), g_k).transpose(0, 2, 1, 3)
    v = v.reshape(b, length, N_ATTN_HEADS, HEAD_DIM).transpose(0, 2, 1, 3)
    y_attn = stick_breaking_attention(q, k, v).transpose(0, 2, 1, 3).reshape(b, length, ATTN_WIDTH)
    y_attn = group_rms_norm(y_attn, g_attn_out, N_ATTN_HEADS)
    return jnp.concatenate([y_conv, y_attn], axis=-1) @ w_out


def squared_relu_mlp(x, g_mlp, w_mlp_in, w_mlp_out):
    hn = rms_norm(x, g_mlp)
    return jnp.square(jax.nn.relu(hn @ w_mlp_in)) @ w_mlp_out


def setup_inputs(seed: int = 0) -> dict:
    key = jax.random.key(seed)
    ks = jax.random.split(key, 13)
    f32 = jnp.float32
    x = jax.random.normal(ks[0], (BATCH, SEQ, D_MODEL), f32)
    meta_tokens = jax.random.normal(ks[1], (N_META, D_MODEL), f32)
    g_mix = 1.0 + 0.02 * jax.random.normal(ks[2], (DEPTH, D_MODEL), f32)
    w_in = jax.random.normal(ks[3], (DEPTH, D_MODEL, IN_PROJ_WIDTH), f32) * D_MODEL ** -0.5
    conv_w = jax.random.normal(ks[4], (DEPTH, CONV_K, CONV_WIDTH), f32) * CONV_K ** -0.5
    g_q = 1.0 + 0.02 * jax.random.normal(ks[5], (DEPTH, HEAD_DIM), f32)
    g_k = 1.0 + 0.02 * jax.random.normal(ks[6], (DEPTH, HEAD_DIM), f32)
    g_conv_out = 1.0 + 0.02 * jax.random.normal(ks[7], (DEPTH, CONV_WIDTH), f32)
    g_attn_out = 1.0 + 0.02 * jax.random.normal(ks[8], (DEPTH, ATTN_WIDTH), f32)
    w_out = jax.random.normal(ks[9], (DEPTH, MIX_WIDTH, D_MODEL), f32) * MIX_WIDTH ** -0.5
    g_mlp = 1.0 + 0.02 * jax.random.normal(ks[10], (DEPTH, D_MODEL), f32)
    w_mlp_in = jax.random.normal(ks[11], (DEPTH, D_MODEL, D_FF), f32) * D_MODEL ** -0.5
    w_mlp_out = jax.random.normal(ks[12], (DEPTH, D_FF, D_MODEL), f32) * D_FF ** -0.5
    return {"x": x, "meta_tokens": meta_tokens, "g_mix": g_mix, "w_in": w_in,
            "conv_w": conv_w, "g_q": g_q, "g_k": g_k, "g_conv_out": g_conv_out,
            "g_attn_out": g_attn_out, "w_out": w_out, "g_mlp": g_mlp,
            "w_mlp_in": w_mlp_in, "w_mlp_out": w_mlp_out}


def reference(x, meta_tokens, g_mix, w_in, conv_w, g_q, g_k, g_conv_out, g_attn_out,
              w_out, g_mlp, w_mlp_in, w_mlp_out):
    b = x.shape[0]
    meta = jnp.broadcast_to(meta_tokens.astype(x.dtype)[None], (b, N_META, x.shape[-1]))
    h = jnp.concatenate([meta, x], axis=1)
    for i in range(DEPTH):
        h = h + hybrid_mixer(h, g_mix[i], w_in[i], conv_w[i], g_q[i], g_k[i],
                             g_conv_out[i], g_attn_out[i], w_out[i])
        h = h + squared_relu_mlp(h, g_mlp[i], w_mlp_in[i], w_mlp_out[i])
    return h[:, N_META:]
```

```python
import numpy as np
import concourse.bass as bass
import concourse.mybir as mybir
from concourse.bass_utils import run_bass_kernel_spmd

F32 = mybir.dt.float32
BF16 = mybir.dt.bfloat16
AF = mybir.ActivationFunctionType
ALU = mybir.AluOpType

D = 2048
KC = 16
DEPTH = 2
PAD = 112
NMETA = 16
SEQ = 4096
DFF = 8192
EPS = 1e-6
NPP = 74
SEQ_STREAMS = False
BQ = "scalar"


class Op:
    __slots__ = ("eng", "fn", "waits", "is_dma", "needs_inc", "count", "sem", "semval", "deps")


class Buf:
    __slots__ = ("name", "w", "r")

    def __init__(self, name=""):
        self.name = name
        self.w = None
        self.r = {}


ENGS = ["tensor", "vector", "scalar", "gpsimd", "sync"]


class Prog:
    def __init__(self, nc, esems, dma_pools):
        self.nc = nc
        self.esem = esems
        self.pools = dma_pools
        self.pool_use = {e: [0] * len(p) for e, p in dma_pools.items()}
        self.pool_last = {e: [None] * len(p) for e, p in dma_pools.items()}
        self.pool_idx = {e: 0 for e in dma_pools}
        self.ops = {e: [] for e in ENGS}
        self.count = {e: 0 for e in ENGS}
        self.waited = {e: {} for e in ENGS}
        self.last_compute = {e: None for e in ENGS}
        self.all_dma = []
        self.nops = 0

    def op(self, eng, fn, reads=(), writes=(), dma=False, nosame=False, extra=(), pool=None):
        o = Op()
        o.eng = eng
        o.fn = fn
        o.is_dma = dma
        o.needs_inc = False
        o.count = None
        o.sem = None
        o.semval = None
        deps = []
        for b in reads:
            if b.w is not None:
                deps.append(b.w)
        for b in writes:
            if b.w is not None:
                deps.append(b.w)
            deps.extend(b.r.values())
        deps.extend(extra)
        fdeps = []
        seen = set()
        for d in deps:
            if id(d) in seen:
                continue
            seen.add(id(d))
            if (not d.is_dma) and d.eng == eng:
                if dma:
                    pass
                elif eng == "tensor" or nosame:
                    continue
            fdeps.append(d)
        o.deps = fdeps
        if dma:
            pk = pool or eng
            pl = self.pools[pk]
            i = self.pool_idx[pk]
            self.pool_idx[pk] = (i + 1) % len(pl)
            prev = self.pool_last[pk][i]
            if prev is not None:
                o.deps.append(prev)
            self.pool_use[pk][i] += 1
            o.sem = pl[i]
            o.semval = 16 * self.pool_use[pk][i]
            self.pool_last[pk][i] = o
            self.all_dma.append(o)
        for d in o.deps:
            if not d.is_dma:
                d.needs_inc = True
        for b in reads:
            key = ("dma", id(o)) if dma else eng
            b.r[key] = o
        for b in writes:
            b.w = o
            b.r = {}
        self.ops[eng].append(o)
        if not dma and fn is not None:
            self.last_compute[eng] = o
        self.nops += 1
        return o

    def barrier(self):
        lasts = [self.last_compute[e] for e in ENGS if self.last_compute[e] is not None]
        dm = list(self.all_dma)
        for e in ENGS:
            self.op(e, None, extra=dm + lasts)
        self.all_dma = []

    def flush(self):
        nc = self.nc
        for e in ENGS:
            c = self.count[e]
            for o in self.ops[e]:
                if not o.is_dma and o.needs_inc:
                    c += 1
                    o.count = c
            self.count[e] = c
        prog = self

        def emit(e_name, eng):
            waited = prog.waited[e_name]
            for o in prog.ops[e_name]:
                for d in o.deps:
                    if d.is_dma:
                        sem, val = d.sem, d.semval
                    else:
                        sem, val = prog.esem[d.eng], d.count
                    key = id(sem)
                    if waited.get(key, 0) < val:
                        eng.wait_ge(sem, val)
                        waited[key] = val
                if o.fn is None:
                    continue
                ins = o.fn(eng)
                if o.is_dma:
                    ins.then_inc(o.sem, 16)
                elif o.needs_inc:
                    ins.then_inc(prog.esem[e_name], 1)

        with nc.Block() as block:
            @block.tensor
            def _(t):
                emit("tensor", t)

            @block.vector
            def _(v):
                emit("vector", v)

            @block.scalar
            def _(s):
                emit("scalar", s)

            @block.gpsimd
            def _(g):
                emit("gpsimd", g)

            @block.sync
            def _(s):
                emit("sync", s)
        self.ops = {e: [] for e in ENGS}


class Rot:
    def __init__(self, aps, name=""):
        self.items = [(a, Buf(name + str(i))) for i, a in enumerate(aps)]
        self.i = 0

    def next(self):
        it = self.items[self.i]
        self.i = (self.i + 1) % len(self.items)
        return it


def build(NT=9):
    S = 128 + 512 * (NT - 1)
    NREAL = S - 128
    TILES = [(0, 128)] + [(128 + 512 * i, 512) for i in range(NT - 1)]
    NKB = S // 128
    QSCALE = float(128 ** -0.5)

    nc = bass.Bass("TRN2", target_bir_lowering=False)
    h0 = nc.dram_tensor("h0", [D, S], F32, kind="ExternalInput").ap()
    pp_d = nc.dram_tensor("pp", [128, DEPTH * NPP], F32, kind="ExternalInput").ap()
    w_in = nc.dram_tensor("w_in", [DEPTH, D, 3 * D], F32, kind="ExternalInput").ap()
    w_out = nc.dram_tensor("w_out", [DEPTH, D, D], F32, kind="ExternalInput").ap()
    w_1 = nc.dram_tensor("w_mlp_in", [DEPTH, D, DFF], F32, kind="ExternalInput").ap()
    w_2 = nc.dram_tensor("w_mlp_out", [DEPTH, DFF, D], F32, kind="ExternalInput").ap()
    outT = nc.dram_tensor("outT", [D, NREAL], F32, kind="ExternalOutput").ap()

    hT = nc.dram_tensor("hT", [D, S], F32).ap()
    wb_in = [nc.dram_tensor(f"wb_in{l}", [12, 128, KC, 512], BF16).ap() for l in range(DEPTH)]
    wb_out = [nc.dram_tensor(f"wb_out{l}", [4, 128, KC, 512], BF16).ap() for l in range(DEPTH)]
    wb_1 = [nc.dram_tensor(f"wb_1{l}", [16, 128, KC, 512], BF16).ap() for l in range(DEPTH)]
    wb_2 = [nc.dram_tensor(f"wb_2{l}", [2, 8, 128, 32, 256], BF16).ap() for l in range(DEPTH)]
    qT_d = nc.dram_tensor("qT", [1024, S], BF16).ap()
    kT_d = nc.dram_tensor("kT", [1024, S], BF16).ap()
    v_d = nc.dram_tensor("v", [S, 1024], BF16).ap()
    mxT_d = nc.dram_tensor("mxT", [D, S], BF16).ap()

    from contextlib import ExitStack
    with ExitStack() as top:
        _cnt = [0]

        def sb(name, shape, dt, stack=top):
            _cnt[0] += 1
            return stack.enter_context(nc.sbuf_tensor(f"{name}_{_cnt[0]}", shape, dt))

        def ps(name, stack=top):
            return stack.enter_context(nc.psum_tensor(name, [128, 512], F32))

        esems = {e: top.enter_context(nc.semaphore("es_" + e)) for e in ENGS}
        pools = {
            "sync": [top.enter_context(nc.semaphore(f"ds{i}")) for i in range(12)],
            "gpsimd": [top.enter_context(nc.semaphore(f"dg{i}")) for i in range(12)],
            "gconv": [top.enter_context(nc.semaphore(f"dc{i}")) for i in range(8)],
            "scalar": [top.enter_context(nc.semaphore(f"da{i}")) for i in range(8)],
        }
        P = Prog(nc, esems, pools)

        pp = sb("pp_sb", [128, DEPTH * NPP], F32)
        ones = sb("ones", [128, 128], BF16)
        tri_in = sb("tri_in", [128, 128], BF16)
        tri_ex = sb("tri_ex", [128, 128], BF16)
        onesf = sb("onesf", [128, 128], F32)
        banks = [ps(f"bank{i}") for i in range(8)]

        b_pp = Buf("pp")
        b_const = Buf("const")
        P.op("sync", lambda e: e.dma_start(out=pp[:], in_=pp_d[:, :]), writes=[b_pp], dma=True)
        P.op("vector", lambda e: e.memset(onesf[:], 1.0), writes=[b_const])
        P.op("vector", lambda e: e.tensor_copy(out=ones[:], in_=onesf[:]), reads=[b_const], writes=[b_const])
        P.op("gpsimd", lambda e: e.affine_select(out=tri_in[:], in_=ones[:], pattern=[[-1, 128]],
                                                 compare_op=ALU.is_ge, fill=0.0, base=0, channel_multiplier=1),
             reads=[b_const], writes=[b_const])
        P.op("gpsimd", lambda e: e.affine_select(out=tri_ex[:], in_=ones[:], pattern=[[1, 128]],
                                                 compare_op=ALU.is_gt, fill=0.0, base=0, channel_multiplier=-1),
             reads=[b_const], writes=[b_const])

        wbufs = {}

        conv_q = {}

        def convert(grp, name, src, dstfn, rows, ncol):
            bl = []
            for c in range(rows // 128):
                b = Buf(f"{name}{c}")

                def rec(c=c, b=b):
                    s_ap = src[c * 128:(c + 1) * 128, :].rearrange("p (j n) -> p j n", n=ncol)
                    d_ap = dstfn(c).rearrange("j p n -> p j n")
                    P.op("gpsimd", lambda e: e.dma_start(out=d_ap, in_=s_ap),
                         writes=[b], dma=True, pool="gconv")
                conv_q.setdefault(grp, []).append(rec)
                bl.append(b)
            wbufs[name] = (bl, 128)

        def emit_conv(grp, n=None):
            q = conv_q.get(grp, [])
            k = len(q) if n is None else min(n, len(q))
            for _ in range(k):
                q.pop(0)()

        convert("in0", "in0", w_in[0], lambda c: wb_in[0][:, :, c, :], D, 512)
        convert("in1", "in1", w_in[1], lambda c: wb_in[1][:, :, c, :], D, 512)
        for l in range(DEPTH):
            convert(f"rest{l}", f"out{l}", w_out[l], lambda c, l=l: wb_out[l][:, :, c, :], D, 512)
            convert(f"rest{l}", f"w1{l}", w_1[l], lambda c, l=l: wb_1[l][:, :, c, :], D, 512)
            convert(f"rest{l}", f"w2{l}", w_2[l], lambda c, l=l: wb_2[l][c // 32, :, :, c % 32, :], DFF, 256)
        emit_conv("in0")

        def wdeps(name, r0, r1):
            bl, rb = wbufs[name]
            return [bl[i] for i in range(r0 // rb, (r1 + rb - 1) // rb)]

        b_h = [[Buf() for _ in range(4)] for t in range(NT)]
        b_q = [[Buf() for _ in range(NT)] for _ in range(8)]
        b_k = [[Buf() for _ in range(NT)] for _ in range(8)]
        b_v = [[[Buf(), Buf()] for _ in range(4)] for t in range(NT)]
        b_mx = [[Buf() for _ in range(NT)] for _ in range(16)]
        out_ops = []

        def ppc(l, off, n=1):
            c = l * NPP + off
            return pp[:, c:c + n]

        O_GMIX, O_GMLP, O_CW, O_GCONV, O_GATTN, O_GQ, O_GK = 0, 16, 32, 56, 64, 72, 73

        def rstd_from(P_, bank_ap, b_bank, lnbuf, rs, scale, tw):
            (ln_ap, b_ln) = lnbuf
            (rs_ap, b_rs) = rs
            P_.op("scalar", lambda e: e.activation(out=ln_ap[:, :tw], in_=bank_ap[:, :tw], func=AF.Ln,
                                                   bias=EPS, scale=scale),
                  reads=[b_bank], writes=[b_ln])
            P_.op("scalar", lambda e: e.activation(out=rs_ap[:, :tw], in_=ln_ap[:, :tw], func=AF.Exp,
                                                   scale=-0.5),
                  reads=[b_ln], writes=[b_rs])

        class _NS:
            pass

        def tile_of(p):
            return 0 if p < 128 else 1 + (p - 128) // 512

        def make_attn(ph, l, zbanks, accbanks, ybanks, stbk):
            A = _NS()
            zb = Rot(zbanks, "z")
            es_ = [Rot([sb("e", [128, 512], F32, ph) for i in range(3)], "e") for s in range(2)]
            sps_ = [Rot([sb("sp", [128, 512], BF16, ph) for i in range(3)], "sp") for s in range(2)]
            tts_ = [Rot([sb("tt", [128, 512], F32, ph) for i in range(2)], "tt") for s in range(2)]
            aas_ = [Rot([sb("aa", [128, 512], BF16, ph) for i in range(3)], "aa") for s in range(2)]
            ysq = Rot([sb("ysq", [128, 512], BF16, ph) for i in range(2)], "ysq")
            lns = Rot([sb("bln", [128, 512], F32, ph) for i in range(2)], "ln")
            rss = Rot([sb("brs", [128, 512], F32, ph) for i in range(2)], "rs")
            osts = Rot([sb("bost", [128, 512], BF16, ph) for i in range(2)], "ost")
            qts = [Rot([sb("qt", [128, 512], BF16, ph) for i in range(2)], "qt") for s in range(2)]
            kvr = [[(sb("kg", [128, 512], BF16, ph), sb("vg", [128, 4, 128], BF16, ph), Buf(), Buf())
                    for i in range(3)] for s in range(2)]
            kvi = [0, 0]
            accs = [(accbanks[s], Buf()) for s in range(2)]
            yks = [(ybanks[s], Buf()) for s in range(2)]
            v_r = v_d.rearrange("(b p) d -> p b d", p=128)

            def stream(h, ti, q0, qw, s):
                nkb = (q0 + qw) // 128
                kbs = list(range(nkb - 1, -1, -1))
                n = len(kbs)
                ngrp = (nkb + 3) // 4
                glist = list(range(ngrp - 1, -1, -1))
                es, sps, tts, aas = es_[s], sps_[s], tts_[s], aas_[s]
                (qt, b_qt) = qts[s].next()
                (acc, b_acc) = accs[s]
                (yk, b_yk) = yks[s]
                P.op(BQ, lambda e: e.dma_start(out=qt[:, :qw], in_=qT_d[128 * h:128 * h + 128, q0:q0 + qw]),
                     reads=[b_q[h][ti]], writes=[b_qt], dma=True)
                slot_of = {}

                def load_group(g):
                    (kg, vg, b_kg, b_vg) = kvr[s][kvi[s]]
                    kvi[s] = (kvi[s] + 1) % 3
                    nb = min(4, nkb - 4 * g)
                    tls = range(tile_of(512 * g), tile_of(512 * g + 128 * nb - 1) + 1)
                    P.op(BQ, lambda e: e.dma_start(out=kg[:, :128 * nb],
                                                       in_=kT_d[128 * h:128 * h + 128, 512 * g:512 * g + 128 * nb]),
                         reads=[b_k[h][t] for t in tls], writes=[b_kg], dma=True)
                    P.op(BQ, lambda e: e.dma_start(out=vg[:, :nb, :],
                                                       in_=v_r[:, 4 * g:4 * g + nb, 128 * h:128 * h + 128]),
                         reads=[b for t in tls for tb in b_v[t] for b in tb], writes=[b_vg], dma=True)
                    slot_of[g] = (kg, vg, b_kg, b_vg)

                nload = 0
                while nload < min(3, ngrp):
                    load_group(glist[nload])
                    nload += 1
                st = {}

                def front1(kb):
                    (kg, vg, b_kg, b_vg) = slot_of[kb // 4]
                    (z, b_z) = zb.next()
                    P.op("tensor", lambda e: e.matmul(z[:, :qw], lhsT=kg[:, 128 * (kb % 4):128 * (kb % 4) + 128],
                                                      rhs=qt[:, :qw], start=True, stop=True),
                         reads=[b_qt, b_kg], writes=[b_z])
                    (ee, b_e) = es.next()
                    P.op("scalar", lambda e: e.activation(out=ee[:, :qw], in_=z[:, :qw], func=AF.Exp, scale=QSCALE),
                         reads=[b_z], writes=[b_e])
                    if kb * 128 >= q0:
                        P.op("gpsimd", lambda e: e.affine_select(
                            out=ee[:, :qw], in_=ee[:, :qw], pattern=[[1, qw]], compare_op=ALU.is_gt,
                            fill=0.0, base=q0 - kb * 128, channel_multiplier=-1),
                             reads=[b_e], writes=[b_e])
                    st[kb] = [ee, b_e, None, None]

                def front2(kb):
                    ee, b_e = st[kb][0], st[kb][1]
                    (sp, b_sp) = sps.next()
                    P.op("scalar", lambda e: e.activation(out=sp[:, :qw], in_=ee[:, :qw], func=AF.Ln, bias=1.0),
                         reads=[b_e], writes=[b_sp])
                    st[kb][2] = sp
                    st[kb][3] = b_sp

                front1(kbs[0])
                front2(kbs[0])
                prev = None
                for ii, kb in enumerate(kbs):
                    (ee, b_e, sp, b_sp) = st.pop(kb)
                    if prev is not None:
                        psp, pb_sp = prev[0], prev[1]
                        P.op("tensor", lambda e, psp=psp: e.matmul(
                            acc[:, :qw], lhsT=tri_ex[:], rhs=psp[:, :qw], start=False, stop=False,
                            skip_group_check=True),
                             reads=[pb_sp, b_const], writes=[b_acc])
                    if ii + 1 < n:
                        front1(kbs[ii + 1])
                    P.op("tensor", lambda e, sp=sp, ii=ii: e.matmul(
                        acc[:, :qw], lhsT=tri_in[:], rhs=sp[:, :qw], start=(ii == 0), stop=(ii == n - 1),
                        skip_group_check=True),
                         reads=[b_sp, b_const], writes=[b_acc])
                    if prev is not None:
                        paa, pb_aa, pkb, pii = prev[2], prev[3], prev[4], prev[5]
                        (kg, vg, b_kg, b_vg) = slot_of[pkb // 4]
                        P.op("tensor", lambda e, paa=paa, vg=vg, pkb=pkb, pii=pii: e.matmul(
                            yk[:, :qw], lhsT=vg[:, pkb % 4, :], rhs=paa[:, :qw], start=(pii == 0), stop=False,
                            skip_group_check=True),
                             reads=[pb_aa, b_vg], writes=[b_yk])
                        if pkb % 4 == 0 and nload < ngrp:
                            load_group(glist[nload])
                            nload += 1
                    (tt, b_tt) = tts.next()
                    P.op("scalar", lambda e, tt=tt: e.activation(out=tt[:, :qw], in_=acc[:, :qw], func=AF.Exp,
                                                                 scale=-1.0),
                         reads=[b_acc], writes=[b_tt])
                    if ii + 1 < n:
                        front2(kbs[ii + 1])
                    (aa, b_aa) = aas.next()
                    P.op("vector", lambda e, aa=aa, ee=ee, tt=tt: e.tensor_tensor(
                        out=aa[:, :qw], in0=ee[:, :qw], in1=tt[:, :qw], op=ALU.mult),
                         reads=[b_e, b_tt], writes=[b_aa])
                    prev = (sp, b_sp, aa, b_aa, kb, ii)
                    yield
                paa, pb_aa, pkb, pii = prev[2], prev[3], prev[4], prev[5]
                (kg, vg, b_kg, b_vg) = slot_of[pkb // 4]
                P.op("tensor", lambda e: e.matmul(
                    yk[:, :qw], lhsT=vg[:, pkb % 4, :], rhs=paa[:, :qw], start=(pii == 0), stop=True,
                    skip_group_check=True),
                     reads=[pb_aa, b_vg], writes=[b_yk])
                (sq, b_sq) = ysq.next()
                P.op("scalar", lambda e: e.activation(out=sq[:, :qw], in_=yk[:, :qw], func=AF.Square),
                     reads=[b_yk], writes=[b_sq])
                yield
                (stb_, b_stb_) = zb.next()
                P.op("tensor", lambda e: e.matmul(stb_[:, :qw], lhsT=ones[:], rhs=sq[:, :qw],
                                                  start=True, stop=True, skip_group_check=True),
                     reads=[b_sq, b_const], writes=[b_stb_])
                lnb = lns.next()
                rsb = rss.next()
                rstd_from(P, stb_, b_stb_, lnb, rsb, 1.0 / 128, qw)
                (ost, b_ost) = osts.next()
                P.op("vector", lambda e: e.scalar_tensor_tensor(
                    out=ost[:, :qw], in0=yk[:, :qw], scalar=ppc(l, O_GATTN + h),
                    in1=rsb[0][:, :qw], op0=ALU.mult, op1=ALU.mult),
                     reads=[b_yk, rsb[1], b_pp], writes=[b_ost])
                P.op("gpsimd", lambda e: e.dma_start(
                    out=mxT_d[1024 + 128 * h:1024 + 128 * h + 128, q0:q0 + qw], in_=ost[:, :qw]),
                     reads=[b_ost], writes=[b_mx[8 + h][ti]], dma=True)
                yield

            A.stream = stream
            return A

        for l in range(DEPTH):
            hsrc = h0 if l == 0 else hT
            hsrc_r = hsrc.rearrange("(c p) t -> p c t", p=128)
            hT_r = hT.rearrange("(c p) t -> p c t", p=128)
            last = (l == DEPTH - 1)

            with ExitStack() as ph:
                ht = sb("ht", [128, KC, 512], F32, ph)
                hn = [sb(f"hn{i}", [128, KC, 512], BF16, ph) for i in range(2)]
                slabs = Rot([sb(f"slab{i}", [128, KC, 512], BF16, ph) for i in range(3)], "slab")
                Bst = sb("Bst", [128, 8, 512], F32, ph)
                Cst = sb("Cst", [128, 8, 512], F32, ph)
                cuc = sb("cuc", [128, 8, 2], F32, ph)
                sqs = Rot([sb(f"sq{i}", [128, 512], BF16, ph) for i in range(4)], "sq")
                sqn = Rot([sb(f"sqn{i}", [128, 512], BF16, ph) for i in range(3)], "sqn")
                lns = Rot([sb(f"ln{i}", [128, 512], F32, ph) for i in range(2)], "ln")
                rss = Rot([sb(f"rs{i}", [128, 512], F32, ph) for i in range(2)], "rs")
                cuxs = Rot([sb(f"cux{i}", [128, 514], F32, ph) for i in range(2)], "cux")
                t1s = Rot([sb(f"t1{i}", [128, 512], F32, ph) for i in range(2)], "t1")
                t2s = Rot([sb(f"t2{i}", [128, 512], F32, ph) for i in range(2)], "t2")
                ys = Rot([sb(f"y{i}", [128, 512], F32, ph) for i in range(2)], "y")
                osts = Rot([sb(f"ost{i}", [128, 512], BF16, ph) for i in range(4)], "ost")
                vsts = Rot([sb(f"vst{i}", [128, 512], BF16, ph) for i in range(2)], "vst")
                rs0 = (sb("rs0", [128, 512], F32, ph), Buf("rs0"))
                ln0 = (sb("ln0", [128, 512], F32, ph), Buf("ln0"))
                b_ht = [Buf(f"ht{i}") for i in range(4)]
                b_hn = [Buf("hn0"), Buf("hn1")]
                b_Bst = [Buf() for _ in range(8)]
                b_Cst = [Buf() for _ in range(8)]
                b_cuc = [Buf() for _ in range(8)]
                mbanks = Rot(banks[0:5], "mb")
                sbanks = Rot(banks[5:8], "sbk")

                P.op("vector", lambda e: e.memset(cuc[:], 0.0), writes=b_cuc)

                def norm_stage(src_ap, b_src, sq, b_sq, gcol, dst, dst_r0, b_dst, t0, tw):
                    (stb2, b_stb2) = sbanks.next()
                    P.op("tensor", lambda e: e.matmul(stb2[:, :tw], lhsT=ones[:], rhs=sq[:, :tw],
                                                      start=True, stop=True),
                         reads=[b_sq, b_const], writes=[b_stb2])
                    lnb = lns.next()
                    rsb = rss.next()
                    rstd_from(P, stb2, b_stb2, lnb, rsb, 1.0 / 128, tw)
                    (ost, b_ost) = osts.next()
                    P.op("vector", lambda e: e.scalar_tensor_tensor(
                        out=ost[:, :tw], in0=src_ap[:, :tw], scalar=gcol,
                        in1=rsb[0][:, :tw], op0=ALU.mult, op1=ALU.mult),
                         reads=[b_src, rsb[1], b_pp], writes=[b_ost])
                    P.op("gpsimd", lambda e: e.dma_start(
                        out=dst[dst_r0:dst_r0 + 128, t0:t0 + tw], in_=ost[:, :tw]),
                         reads=[b_ost], writes=[b_dst], dma=True)

                def conv_chunk(g, mb, b_mb, t0, tw, ti, pending):
                    (cux, b_cux) = cuxs.next()
                    (t1, b_t1) = t1s.next()
                    (t2, b_t2) = t2s.next()
                    (y, b_y) = ys.next()
                    (sq, b_sq) = sqs.next()
                    P.op("scalar", lambda e: e.activation(out=cux[:, 0:2], in_=cuc[:, g, :], func=AF.Copy),
                         reads=[b_cuc[g]], writes=[b_cux])
                    P.op("vector", lambda e: e.tensor_tensor(
                        out=cux[:, 2:2 + tw], in0=mb[:, :tw], in1=Cst[:, g, :tw], op=ALU.mult),
                         reads=[b_mb, b_Cst[g]], writes=[b_cux])
                    P.op("scalar", lambda e: e.activation(out=cuc[:, g, :], in_=cux[:, tw:tw + 2], func=AF.Copy),
                         reads=[b_cux], writes=[b_cuc[g]])
                    P.op("vector", lambda e: e.tensor_scalar_mul(
                        out=t1[:, :tw], in0=cux[:, 2:2 + tw], scalar1=ppc(l, O_CW + 16 + g)),
                         reads=[b_cux, b_pp], writes=[b_t1])
                    P.op("vector", lambda e: e.scalar_tensor_tensor(
                        out=t2[:, :tw], in0=cux[:, 1:1 + tw], scalar=ppc(l, O_CW + 8 + g),
                        in1=t1[:, :tw], op0=ALU.mult, op1=ALU.add),
                         reads=[b_cux, b_t1, b_pp], writes=[b_t2])
                    P.op("vector", lambda e: e.scalar_tensor_tensor(
                        out=t1[:, :tw], in0=cux[:, 0:tw], scalar=ppc(l, O_CW + g),
                        in1=t2[:, :tw], op0=ALU.mult, op1=ALU.add),
                         reads=[b_cux, b_t2, b_pp], writes=[b_t1])
                    P.op("vector", lambda e: e.tensor_tensor(
                        out=y[:, :tw], in0=t1[:, :tw], in1=Bst[:, g, :tw], op=ALU.mult),
                         reads=[b_t1, b_Bst[g]], writes=[b_y])
                    P.op("scalar", lambda e: e.activation(out=sq[:, :tw], in_=y[:, :tw], func=AF.Square),
                         reads=[b_y], writes=[b_sq])
                    pending.append((0, lambda: norm_stage(y, b_y, sq, b_sq, ppc(l, O_GCONV + g), mxT_d, 128 * g,
                                                          b_mx[g][ti], t0, tw)))

                def qk_chunk(isq, hh, mb, b_mb, t0, tw, ti, pending):
                    (sq, b_sq) = sqs.next()
                    P.op("scalar", lambda e: e.activation(out=sq[:, :tw], in_=mb[:, :tw], func=AF.Square),
                         reads=[b_mb], writes=[b_sq])
                    dst = qT_d if isq else kT_d
                    bb = b_q if isq else b_k
                    pending.append((0, lambda: norm_stage(mb, b_mb, sq, b_sq, ppc(l, O_GQ if isq else O_GK), dst,
                                                          128 * hh, bb[hh][ti], t0, tw)))

                def mm_chunk(mb, b_mb, slab, b_slab, i, hn_t, bhn, tw):
                    for k in range(KC):
                        P.op("tensor", lambda e, k=k: e.matmul(
                            mb[:, :tw], lhsT=slab[:, k, 128 * i:128 * i + 128], rhs=hn_t[:, k, :tw],
                            start=(k == 0), stop=(k == KC - 1)),
                             reads=[b_slab, bhn], writes=[b_mb])

                def mm_chunk_v(mb, b_mb, slab, b_slab, tb, hn_t, bhn):
                    for k in range(KC):
                        P.op("tensor", lambda e, k=k: e.matmul(
                            mb[:, :], lhsT=hn_t[:, k, 128 * tb:128 * tb + 128], rhs=slab[:, k, :],
                            start=(k == 0), stop=(k == KC - 1)),
                             reads=[b_slab, bhn], writes=[b_mb])

                def load_slab(slab, b_slab, src_ap, deps):
                    P.op("sync", lambda e: e.dma_start(out=slab[:], in_=src_ap),
                         reads=deps, writes=[b_slab], dma=True)

                def normA_tile(ti, t0, tw):
                    hslot = ti % 2
                    hn_t = hn[hslot]
                    bhn = b_hn[hslot]
                    for g4 in range(4):
                        P.op("sync", lambda e, g4=g4: e.dma_start(out=ht[:, 4 * g4:4 * g4 + 4, :tw],
                                                                  in_=hsrc_r[:, 4 * g4:4 * g4 + 4, t0:t0 + tw]),
                             reads=[b_h[ti][g4]], writes=[b_ht[g4]], dma=True)
                    (stb, b_stb) = sbanks.next()
                    for c in range(KC):
                        (sq, b_sq) = sqn.next()
                        P.op("scalar", lambda e, c=c, sq=sq: e.activation(out=sq[:, :tw], in_=ht[:, c, :tw],
                                                                          func=AF.Square),
                             reads=[b_ht[c // 4]], writes=[b_sq])
                        P.op("tensor", lambda e, c=c, sq=sq: e.matmul(
                            stb[:, :tw], lhsT=ones[:], rhs=sq[:, :tw], start=(c == 0), stop=(c == KC - 1)),
                             reads=[b_sq, b_const], writes=[b_stb])
                    rstd_from(P, stb, b_stb, ln0, rs0, 1.0 / D, tw)
                    for c in range(KC):
                        P.op("vector", lambda e, c=c: e.scalar_tensor_tensor(
                            out=hn_t[:, c, :tw], in0=ht[:, c, :tw], scalar=ppc(l, O_GMIX + c),
                            in1=rs0[0][:, :tw], op0=ALU.mult, op1=ALU.mult),
                             reads=[b_ht[c // 4], rs0[1], b_pp], writes=[bhn], nosame=True)

                def phaseA_tile(ti, t0, tw):
                    hslot = ti % 2
                    hn_t = hn[hslot]
                    bhn = b_hn[hslot]
                    pending = []

                    def run_pending(force=False):
                        cur = list(pending)
                        pending[:] = []
                        for (age, fn) in cur:
                            if age >= 1 or force:
                                fn()
                            else:
                                pending.append((age + 1, fn))

                    for j in range(10):
                        (slab, b_slab) = slabs.next()
                        load_slab(slab, b_slab, wb_in[l][j], wdeps(f"in{l}", 0, D))
                        for i in range(4):
                            oc = 4 * j + i
                            (mb, b_mb) = mbanks.next()
                            mm_chunk(mb, b_mb, slab, b_slab, i, hn_t, bhn, tw)
                            run_pending()
                            if oc < 8:
                                P.op("scalar", lambda e, mb=mb, g=oc: e.activation(
                                    out=Bst[:, g, :tw], in_=mb[:, :tw], func=AF.Copy),
                                     reads=[b_mb], writes=[b_Bst[oc]])
                            elif oc < 16:
                                P.op("scalar", lambda e, mb=mb, g=oc - 8: e.activation(
                                    out=Cst[:, g, :tw], in_=mb[:, :tw], func=AF.Copy),
                                     reads=[b_mb], writes=[b_Cst[oc - 8]])
                            elif oc < 24:
                                conv_chunk(oc - 16, mb, b_mb, t0, tw, ti, pending)
                            else:
                                isq = oc < 32
                                qk_chunk(isq, (oc - 24) if isq else (oc - 32), mb, b_mb, t0, tw, ti, pending)
                        if j == 6 and ti + 1 < NT:
                            normA_tile(ti + 1, TILES[ti + 1][0], TILES[ti + 1][1])
                    for j in range(10, 12):
                        (slab, b_slab) = slabs.next()
                        load_slab(slab, b_slab, wb_in[l][j], wdeps(f"in{l}", 0, D))
                        for tb in range(tw // 128):
                            (mb, b_mb) = mbanks.next()
                            mm_chunk_v(mb, b_mb, slab, b_slab, tb, hn_t, bhn)
                            run_pending()
                            (vst, b_vst) = vsts.next()
                            P.op("scalar", lambda e, mb=mb, vst=vst: e.activation(
                                out=vst[:, :], in_=mb[:, :], func=AF.Copy),
                                 reads=[b_mb], writes=[b_vst])
                            P.op("gpsimd", lambda e, vst=vst, tb=tb, j=j: e.dma_start(
                                out=v_d[t0 + 128 * tb:t0 + 128 * tb + 128, 512 * (j - 10):512 * (j - 10) + 512],
                                in_=vst[:, :]),
                                 reads=[b_vst], writes=[b_v[ti][tb][j - 10]], dma=True)
                    run_pending(force=True)
                    run_pending(force=True)

                normA_tile(0, TILES[0][0], TILES[0][1])
                for ti, (t0, tw) in enumerate(TILES):
                    phaseA_tile(ti, t0, tw)
                    if l == 0:
                        emit_conv("rest0", 11)
                if l == 0:
                    emit_conv("rest0")
                P.barrier()
                P.flush()

            with ExitStack() as ph:
                stb_shared = (banks[7], Buf("stb"))
                AT = make_attn(ph, l, [banks[0]], [banks[1], banks[2]], [banks[3], banks[4]], stb_shared)
                ht = sb("cht", [128, KC, 512], F32, ph)
                mx = sb("cmx", [128, KC, 512], BF16, ph)
                hn2 = mx
                hid = sb("chid", [128, 32, 512], BF16, ph)
                slab_t = [sb(f"cslab{i}", [128, 8192], BF16, ph) for i in range(3)]
                slabs = Rot(slab_t, "slab")
                sqs = Rot([sb(f"csq{i}", [128, 512], BF16, ph) for i in range(3)], "sq")
                rls = Rot([sb(f"crl{i}", [128, 512], F32, ph) for i in range(2)], "rl")
                rs0 = (sb("crs0", [128, 512], F32, ph), Buf("rs0"))
                ln0 = (sb("cln0", [128, 512], F32, ph), Buf("ln0"))
                b_ht = [Buf() for _ in range(KC)]
                b_mxs = [Buf() for _ in range(4)]
                b_hid = [Buf() for _ in range(32)]
                mbanks = Rot(banks[5:7], "mb")
                stbk = stb_shared
                mx_r = mxT_d.rearrange("(c p) t -> p c t", p=128)
                o_r = outT.rearrange("(c p) t -> p c t", p=128)

                def load_slab_c(slab, b_slab, ncol, src_ap, deps):
                    P.op("sync", lambda e: e.dma_start(out=slab[:, :], in_=src_ap.rearrange("p k n -> p (k n)")),
                         reads=deps, writes=[b_slab], dma=True)

                def phaseC_gen(ti, t0, tw):
                    for g4 in range(4):
                        P.op("sync", lambda e, g4=g4: e.dma_start(
                            out=ht[:, 4 * g4:4 * g4 + 4, :tw], in_=hsrc_r[:, 4 * g4:4 * g4 + 4, t0:t0 + tw]),
                             reads=[b_h[ti][g4]], writes=b_ht[4 * g4:4 * g4 + 4], dma=True)
                    for g4 in range(4):
                        P.op("sync", lambda e, g4=g4: e.dma_start(
                            out=mx[:, 4 * g4:4 * g4 + 4, :tw], in_=mx_r[:, 4 * g4:4 * g4 + 4, t0:t0 + tw]),
                             reads=[b_mx[c][ti] for c in range(4 * g4, 4 * g4 + 4)], writes=[b_mxs[g4]], dma=True)
                    for j in range(4):
                        (slab, b_slab) = slabs.next()
                        load_slab_c(slab, b_slab, 512, wb_out[l][j], wdeps(f"out{l}", 0, D))
                        for i in range(4):
                            oc = 4 * j + i
                            (mb, b_mb) = mbanks.next()
                            for k in range(KC):
                                P.op("tensor", lambda e, mb=mb, slab=slab, i=i, k=k: e.matmul(
                                    mb[:, :tw], lhsT=slab[:, 512 * k + 128 * i:512 * k + 128 * i + 128],
                                    rhs=mx[:, k, :tw], start=(k == 0), stop=(k == KC - 1), skip_group_check=True),
                                     reads=[b_slab, b_mxs[k // 4]], writes=[b_mb])
                            P.op("vector", lambda e, mb=mb, oc=oc: e.tensor_tensor(
                                out=ht[:, oc, :tw], in0=mb[:, :tw], in1=ht[:, oc, :tw], op=ALU.add),
                                 reads=[b_mb, b_ht[oc]], writes=[b_ht[oc]])
                            (sq, b_sq) = sqs.next()
                            P.op("scalar", lambda e, sq=sq, oc=oc: e.activation(
                                out=sq[:, :tw], in_=ht[:, oc, :tw], func=AF.Square),
                                 reads=[b_ht[oc]], writes=[b_sq])
                            yield
                            P.op("tensor", lambda e, sq=sq, oc=oc: e.matmul(
                                stbk[0][:, :tw], lhsT=ones[:], rhs=sq[:, :tw], start=(oc == 0), stop=(oc == KC - 1),
                                skip_group_check=True),
                                 reads=[b_sq, b_const], writes=[stbk[1]])
                    rstd_from(P, stbk[0], stbk[1], ln0, rs0, 1.0 / D, tw)
                    for c in range(KC):
                        P.op("vector", lambda e, c=c: e.scalar_tensor_tensor(
                            out=hn2[:, c, :tw], in0=ht[:, c, :tw], scalar=ppc(l, O_GMLP + c),
                            in1=rs0[0][:, :tw], op0=ALU.mult, op1=ALU.mult),
                             reads=[b_ht[c], rs0[1], b_pp], writes=[b_mxs[c // 4]], nosame=True)
                    for half in range(2):
                        for j in range(8 * half, 8 * half + 8):
                            (slab, b_slab) = slabs.next()
                            load_slab_c(slab, b_slab, 512, wb_1[l][j], wdeps(f"w1{l}", 0, D))
                            for i in range(4):
                                f = 4 * j + i - 32 * half
                                (mb, b_mb) = mbanks.next()
                                for k in range(KC):
                                    P.op("tensor", lambda e, mb=mb, slab=slab, i=i, k=k: e.matmul(
                                        mb[:, :tw], lhsT=slab[:, 512 * k + 128 * i:512 * k + 128 * i + 128],
                                        rhs=hn2[:, k, :tw], start=(k == 0), stop=(k == KC - 1), skip_group_check=True),
                                         reads=[b_slab, b_mxs[k // 4]], writes=[b_mb])
                                (rl, b_rl) = rls.next()
                                P.op("scalar", lambda e, mb=mb, rl=rl: e.activation(
                                    out=rl[:, :tw], in_=mb[:, :tw], func=AF.Relu),
                                     reads=[b_mb], writes=[b_rl])
                                P.op("vector", lambda e, mb=mb, rl=rl, f=f: e.tensor_tensor(
                                    out=hid[:, f, :tw], in0=mb[:, :tw], in1=rl[:, :tw], op=ALU.mult),
                                     reads=[b_mb, b_rl], writes=[b_hid[f]])
                                yield
                        for j2 in range(8):
                            (slab, b_slab) = slabs.next()
                            load_slab_c(slab, b_slab, 256, wb_2[l][half, j2],
                                        wdeps(f"w2{l}", 4096 * half, 4096 * half + 4096))
                            for i in range(2):
                                oc = 2 * j2 + i
                                (mb, b_mb) = mbanks.next()
                                for k in range(32):
                                    P.op("tensor", lambda e, mb=mb, slab=slab, i=i, k=k: e.matmul(
                                        mb[:, :tw], lhsT=slab[:, 256 * k + 128 * i:256 * k + 128 * i + 128],
                                        rhs=hid[:, k, :tw], start=(k == 0), stop=(k == 31), skip_group_check=True),
                                         reads=[b_slab, b_hid[k]], writes=[b_mb])
                                    if k == 15:
                                        yield
                                P.op("vector", lambda e, mb=mb, oc=oc: e.tensor_tensor(
                                    out=ht[:, oc, :tw], in0=mb[:, :tw], in1=ht[:, oc, :tw], op=ALU.add),
                                     reads=[b_mb, b_ht[oc]], writes=[b_ht[oc]])
                                yield
                            if half == 1 and j2 % 2 == 1:
                                j = j2 // 2
                                if last:
                                    P.op("gpsimd", lambda e, j=j: e.dma_start(
                                        out=o_r[:, 4 * j:4 * j + 4, t0 - 128:t0 - 128 + tw],
                                        in_=ht[:, 4 * j:4 * j + 4, :tw]),
                                         reads=b_ht[4 * j:4 * j + 4], dma=True)
                                else:
                                    P.op("gpsimd", lambda e, j=j: e.dma_start(
                                        out=hT_r[:, 4 * j:4 * j + 4, t0:t0 + tw], in_=ht[:, 4 * j:4 * j + 4, :tw]),
                                         reads=b_ht[4 * j:4 * j + 4], writes=[b_h[ti][j]], dma=True)

                def step(g):
                    try:
                        next(g)
                        return True
                    except StopIteration:
                        return False

                def b_streams(ti):
                    (q0, qw) = TILES[ti]
                    return [(h, ti, q0, qw) for h in range(8)]

                def run_merged(cgen, bspecs):
                    pend = list(bspecs)
                    act = [None, None]
                    c_alive = cgen is not None
                    while c_alive or pend or act[0] is not None or act[1] is not None:
                        for s in range(2):
                            if act[s] is None and pend:
                                (h, ti, q0, qw) = pend.pop(0)
                                act[s] = AT.stream(h, ti, q0, qw, s)
                            if act[s] is not None:
                                if not step(act[s]):
                                    act[s] = None
                        if c_alive:
                            c_alive = step(cgen)

                run_merged(None, b_streams(0))
                for ti, (t0, tw) in enumerate(TILES):
                    cgen = None if (last and ti == 0) else phaseC_gen(ti, t0, tw)
                    bs = b_streams(ti + 1) if ti + 1 < NT else []
                    run_merged(cgen, bs)
                    if l == 0:
                        emit_conv("in1", 2)
                        emit_conv("rest1", 11)
                if l == 0:
                    emit_conv("in1")
                    emit_conv("rest1")
                P.barrier()
                P.flush()
    return nc


_NC_CACHE = {}


def _prep_inputs(inputs, NT=9):
    S = 128 + 512 * (NT - 1)
    nreal = S - 128
    x = np.asarray(inputs["x"], dtype=np.float32)
    meta = np.asarray(inputs["meta_tokens"], dtype=np.float32)
    B = x.shape[0]
    pp = np.zeros((128, DEPTH * NPP), np.float32)
    for l in range(DEPTH):
        o = l * NPP
        pp[:, o + 0:o + 16] = np.asarray(inputs["g_mix"][l]).reshape(16, 128).T
        pp[:, o + 16:o + 32] = np.asarray(inputs["g_mlp"][l]).reshape(16, 128).T
        cw = np.asarray(inputs["conv_w"][l])
        for k in range(3):
            pp[:, o + 32 + 8 * k:o + 32 + 8 * k + 8] = cw[k].reshape(8, 128).T
        pp[:, o + 56:o + 64] = np.asarray(inputs["g_conv_out"][l]).reshape(8, 128).T
        pp[:, o + 64:o + 72] = np.asarray(inputs["g_attn_out"][l]).reshape(8, 128).T
        pp[:, o + 72] = np.asarray(inputs["g_q"][l])
        pp[:, o + 73] = np.asarray(inputs["g_k"][l])
    shared = {
        "pp": pp,
        "w_in": np.ascontiguousarray(inputs["w_in"], dtype=np.float32),
        "w_out": np.ascontiguousarray(inputs["w_out"], dtype=np.float32),
        "w_mlp_in": np.ascontiguousarray(inputs["w_mlp_in"], dtype=np.float32),
        "w_mlp_out": np.ascontiguousarray(inputs["w_mlp_out"], dtype=np.float32),
    }
    in_maps = []
    for b in range(B):
        h0 = np.zeros((D, S), np.float32)
        h0[:, PAD:PAD + NMETA] = meta.T
        h0[:, 128:] = x[b, :nreal].T
        m = dict(shared)
        m["h0"] = h0
        in_maps.append(m)
    return in_maps


def kernel(**inputs):
    NT = 9
    if NT not in _NC_CACHE:
        _NC_CACHE[NT] = build(NT)
    nc = _NC_CACHE[NT]
    in_maps = _prep_inputs(inputs, NT)
    res = run_bass_kernel_spmd(nc, in_maps, core_ids=list(range(8)))
    out = np.stack([np.ascontiguousarray(r["outT"].T) for r in res.results], axis=0)
    return out.astype(np.float32)
```

```python
import numpy as np
import concourse.bass as bass
import concourse.mybir as mybir
from concourse.bass_utils import run_bass_kernel_spmd

F32 = mybir.dt.float32
BF16 = mybir.dt.bfloat16
AF = mybir.ActivationFunctionType
ALU = mybir.AluOpType

D = 2048
KC = 16
DEPTH = 2
PAD = 112
NMETA = 16
SEQ = 4096
DFF = 8192
EPS = 1e-6
NPP = 74
SEQ_STREAMS = False
BQ = "sync"


class Op:
    __slots__ = ("eng", "fn", "waits", "is_dma", "needs_inc", "count", "sem", "semval", "deps")


class Buf:
    __slots__ = ("name", "w", "r")

    def __init__(self, name=""):
        self.name = name
        self.w = None
        self.r = {}


ENGS = ["tensor", "vector", "scalar", "gpsimd", "sync"]


class Prog:
    def __init__(self, nc, esems, dma_pools):
        self.nc = nc
        self.esem = esems
        self.pools = dma_pools
        self.pool_use = {e: [0] * len(p) for e, p in dma_pools.items()}
        self.pool_last = {e: [None] * len(p) for e, p in dma_pools.items()}
        self.pool_idx = {e: 0 for e in dma_pools}
        self.ops = {e: [] for e in ENGS}
        self.count = {e: 0 for e in ENGS}
        self.waited = {e: {} for e in ENGS}
        self.last_compute = {e: None for e in ENGS}
        self.all_dma = []
        self.nops = 0

    def op(self, eng, fn, reads=(), writes=(), dma=False, nosame=False, extra=(), pool=None):
        o = Op()
        o.eng = eng
        o.fn = fn
        o.is_dma = dma
        o.needs_inc = False
        o.count = None
        o.sem = None
        o.semval = None
        deps = []
        for b in reads:
            if b.w is not None:
                deps.append(b.w)
        for b in writes:
            if b.w is not None:
                deps.append(b.w)
            deps.extend(b.r.values())
        deps.extend(extra)
        fdeps = []
        seen = set()
        for d in deps:
            if id(d) in seen:
                continue
            seen.add(id(d))
            if (not d.is_dma) and d.eng == eng:
                if dma:
                    pass
                elif eng == "tensor" or nosame:
                    continue
            fdeps.append(d)
        o.deps = fdeps
        if dma:
            pk = pool or eng
            pl = self.pools[pk]
            i = self.pool_idx[pk]
            self.pool_idx[pk] = (i + 1) % len(pl)
            prev = self.pool_last[pk][i]
            if prev is not None:
                o.deps.append(prev)
            self.pool_use[pk][i] += 1
            o.sem = pl[i]
            o.semval = 16 * self.pool_use[pk][i]
            self.pool_last[pk][i] = o
            self.all_dma.append(o)
        for d in o.deps:
            if not d.is_dma:
                d.needs_inc = True
        for b in reads:
            key = ("dma", id(o)) if dma else eng
            b.r[key] = o
        for b in writes:
            b.w = o
            b.r = {}
        self.ops[eng].append(o)
        if not dma and fn is not None:
            self.last_compute[eng] = o
        self.nops += 1
        return o

    def barrier(self):
        lasts = [self.last_compute[e] for e in ENGS if self.last_compute[e] is not None]
        dm = list(self.all_dma)
        for e in ENGS:
            self.op(e, None, extra=dm + lasts)
        self.all_dma = []

    def flush(self):
        nc = self.nc
        for e in ENGS:
            c = self.count[e]
            for o in self.ops[e]:
                if not o.is_dma and o.needs_inc:
                    c += 1
                    o.count = c
            self.count[e] = c
        prog = self

        def emit(e_name, eng):
            waited = prog.waited[e_name]
            for o in prog.ops[e_name]:
                for d in o.deps:
                    if d.is_dma:
                        sem, val = d.sem, d.semval
                    else:
                        sem, val = prog.esem[d.eng], d.count
                    key = id(sem)
                    if waited.get(key, 0) < val:
                        eng.wait_ge(sem, val)
                        waited[key] = val
                if o.fn is None:
                    continue
                ins = o.fn(eng)
                if o.is_dma:
                    ins.then_inc(o.sem, 16)
                elif o.needs_inc:
                    ins.then_inc(prog.esem[e_name], 1)

        with nc.Block() as block:
            @block.tensor
            def _(t):
                emit("tensor", t)

            @block.vector
            def _(v):
                emit("vector", v)

            @block.scalar
            def _(s):
                emit("scalar", s)

            @block.gpsimd
            def _(g):
                emit("gpsimd", g)

            @block.sync
            def _(s):
                emit("sync", s)
        self.ops = {e: [] for e in ENGS}


class Rot:
    def __init__(self, aps, name=""):
        self.items = [(a, Buf(name + str(i))) for i, a in enumerate(aps)]
        self.i = 0

    def next(self):
        it = self.items[self.i]
        self.i = (self.i + 1) % len(self.items)
        return it


def build(NT=9):
    S = 128 + 512 * (NT - 1)
    NREAL = S - 128
    TILES = [(0, 128)] + [(128 + 512 * i, 512) for i in range(NT - 1)]
    NKB = S // 128
    QSCALE = float(128 ** -0.5)

    nc = bass.Bass("TRN2", target_bir_lowering=False)
    h0 = nc.dram_tensor("h0", [D, S], F32, kind="ExternalInput").ap()
    pp_d = nc.dram_tensor("pp", [128, DEPTH * NPP], F32, kind="ExternalInput").ap()
    w_in = nc.dram_tensor("w_in", [DEPTH, D, 3 * D], F32, kind="ExternalInput").ap()
    w_out = nc.dram_tensor("w_out", [DEPTH, D, D], F32, kind="ExternalInput").ap()
    w_1 = nc.dram_tensor("w_mlp_in", [DEPTH, D, DFF], F32, kind="ExternalInput").ap()
    w_2 = nc.dram_tensor("w_mlp_out", [DEPTH, DFF, D], F32, kind="ExternalInput").ap()
    outT = nc.dram_tensor("outT", [D, NREAL], F32, kind="ExternalOutput").ap()

    hT = nc.dram_tensor("hT", [D, S], F32).ap()
    wb_in = [nc.dram_tensor(f"wb_in{l}", [12, 128, KC, 512], BF16).ap() for l in range(DEPTH)]
    wb_out = [nc.dram_tensor(f"wb_out{l}", [4, 128, KC, 512], BF16).ap() for l in range(DEPTH)]
    wb_1 = [nc.dram_tensor(f"wb_1{l}", [16, 128, KC, 512], BF16).ap() for l in range(DEPTH)]
    wb_2 = [nc.dram_tensor(f"wb_2{l}", [2, 8, 128, 32, 256], BF16).ap() for l in range(DEPTH)]
    qT_d = nc.dram_tensor("qT", [1024, S], BF16).ap()
    kT_d = nc.dram_tensor("kT", [1024, S], BF16).ap()
    v_d = nc.dram_tensor("v", [S, 1024], BF16).ap()
    mxT_d = nc.dram_tensor("mxT", [D, S], BF16).ap()

    from contextlib import ExitStack
    with ExitStack() as top:
        _cnt = [0]

        def sb(name, shape, dt, stack=top):
            _cnt[0] += 1
            return stack.enter_context(nc.sbuf_tensor(f"{name}_{_cnt[0]}", shape, dt))

        def ps(name, stack=top):
            return stack.enter_context(nc.psum_tensor(name, [128, 512], F32))

        esems = {e: top.enter_context(nc.semaphore("es_" + e)) for e in ENGS}
        pools = {
            "sync": [top.enter_context(nc.semaphore(f"ds{i}")) for i in range(12)],
            "gpsimd": [top.enter_context(nc.semaphore(f"dg{i}")) for i in range(12)],
            "gconv": [top.enter_context(nc.semaphore(f"dc{i}")) for i in range(8)],
            "scalar": [top.enter_context(nc.semaphore(f"da{i}")) for i in range(8)],
        }
        P = Prog(nc, esems, pools)

        pp = sb("pp_sb", [128, DEPTH * NPP], F32)
        ones = sb("ones", [128, 128], BF16)
        tri_in = sb("tri_in", [128, 128], BF16)
        tri_ex = sb("tri_ex", [128, 128], BF16)
        onesf = sb("onesf", [128, 128], F32)
        banks = [ps(f"bank{i}") for i in range(8)]

        b_pp = Buf("pp")
        b_const = Buf("const")
        P.op("sync", lambda e: e.dma_start(out=pp[:], in_=pp_d[:, :]), writes=[b_pp], dma=True)
        P.op("vector", lambda e: e.memset(onesf[:], 1.0), writes=[b_const])
        P.op("vector", lambda e: e.tensor_copy(out=ones[:], in_=onesf[:]), reads=[b_const], writes=[b_const])
        P.op("gpsimd", lambda e: e.affine_select(out=tri_in[:], in_=ones[:], pattern=[[-1, 128]],
                                                 compare_op=ALU.is_ge, fill=0.0, base=0, channel_multiplier=1),
             reads=[b_const], writes=[b_const])
        P.op("gpsimd", lambda e: e.affine_select(out=tri_ex[:], in_=ones[:], pattern=[[1, 128]],
                                                 compare_op=ALU.is_gt, fill=0.0, base=0, channel_multiplier=-1),
             reads=[b_const], writes=[b_const])

        wbufs = {}

        conv_q = {}

        def convert(grp, name, src, dstfn, rows, ncol):
            bl = []
            for c in range(rows // 128):
                b = Buf(f"{name}{c}")

                def rec(c=c, b=b):
                    s_ap = src[c * 128:(c + 1) * 128, :].rearrange("p (j n) -> p j n", n=ncol)
                    d_ap = dstfn(c).rearrange("j p n -> p j n")
                    P.op("gpsimd", lambda e: e.dma_start(out=d_ap, in_=s_ap),
                         writes=[b], dma=True, pool="gconv")
                conv_q.setdefault(grp, []).append(rec)
                bl.append(b)
            wbufs[name] = (bl, 128)

        def emit_conv(grp, n=None):
            q = conv_q.get(grp, [])
            k = len(q) if n is None else min(n, len(q))
            for _ in range(k):
                q.pop(0)()

        convert("in0", "in0", w_in[0], lambda c: wb_in[0][:, :, c, :], D, 512)
        convert("in1", "in1", w_in[1], lambda c: wb_in[1][:, :, c, :], D, 512)
        for l in range(DEPTH):
            convert(f"rest{l}", f"out{l}", w_out[l], lambda c, l=l: wb_out[l][:, :, c, :], D, 512)
            convert(f"rest{l}", f"w1{l}", w_1[l], lambda c, l=l: wb_1[l][:, :, c, :], D, 512)
            convert(f"rest{l}", f"w2{l}", w_2[l], lambda c, l=l: wb_2[l][c // 32, :, :, c % 32, :], DFF, 256)
        emit_conv("in0")

        def wdeps(name, r0, r1):
            bl, rb = wbufs[name]
            return [bl[i] for i in range(r0 // rb, (r1 + rb - 1) // rb)]

        b_h = [[Buf() for _ in range(4)] for t in range(NT)]
        b_q = [[Buf() for _ in range(NT)] for _ in range(8)]
        b_k = [[Buf() for _ in range(NT)] for _ in range(8)]
        b_v = [[[Buf(), Buf()] for _ in range(4)] for t in range(NT)]
        b_mx = [[Buf() for _ in range(NT)] for _ in range(16)]
        out_ops = []

        def ppc(l, off, n=1):
            c = l * NPP + off
            return pp[:, c:c + n]

        O_GMIX, O_GMLP, O_CW, O_GCONV, O_GATTN, O_GQ, O_GK = 0, 16, 32, 56, 64, 72, 73

        def rstd_from(P_, bank_ap, b_bank, lnbuf, rs, scale, tw):
            (ln_ap, b_ln) = lnbuf
            (rs_ap, b_rs) = rs
            P_.op("scalar", lambda e: e.activation(out=ln_ap[:, :tw], in_=bank_ap[:, :tw], func=AF.Ln,
                                                   bias=EPS, scale=scale),
                  reads=[b_bank], writes=[b_ln])
            P_.op("scalar", lambda e: e.activation(out=rs_ap[:, :tw], in_=ln_ap[:, :tw], func=AF.Exp,
                                                   scale=-0.5),
                  reads=[b_ln], writes=[b_rs])

        class _NS:
            pass

        def tile_of(p):
            return 0 if p < 128 else 1 + (p - 128) // 512

        def make_attn(ph, l, zbanks, accbanks, ybanks, stbk):
            A = _NS()
            zb = Rot(zbanks, "z")
            es_ = [Rot([sb("e", [128, 512], F32, ph) for i in range(2)], "e") for s in range(2)]
            sps_ = [Rot([sb("sp", [128, 512], BF16, ph) for i in range(3)], "sp") for s in range(2)]
            tts_ = [Rot([sb("tt", [128, 512], F32, ph) for i in range(2)], "tt") for s in range(2)]
            aas_ = [Rot([sb("aa", [128, 512], BF16, ph) for i in range(3)], "aa") for s in range(2)]
            ysq = Rot([sb("ysq", [128, 512], BF16, ph) for i in range(2)], "ysq")
            lns = Rot([sb("bln", [128, 512], F32, ph) for i in range(1)], "ln")
            rss = Rot([sb("brs", [128, 512], F32, ph) for i in range(1)], "rs")
            osts = Rot([sb("bost", [128, 512], BF16, ph) for i in range(2)], "ost")
            qts = [Rot([sb("qt", [128, 512], BF16, ph) for i in range(2)], "qt") for s in range(2)]
            NRING = 5
            kvr = [[(sb("kg", [128, 512], BF16, ph), sb("vg", [128, 4, 128], BF16, ph), Buf(), Buf())
                    for i in range(NRING)] for s in range(2)]
            kvi = [0, 0]
            accs = [(accbanks[s], Buf()) for s in range(2)]
            yks = [(ybanks[s], Buf()) for s in range(2)]
            v_r = v_d.rearrange("(b p) d -> p b d", p=128)

            def stream(h, ti, q0, qw, s):
                nkb = (q0 + qw) // 128
                kbs = list(range(nkb - 1, -1, -1))
                n = len(kbs)
                ngrp = (nkb + 3) // 4
                glist = list(range(ngrp - 1, -1, -1))
                es, sps, tts, aas = es_[s], sps_[s], tts_[s], aas_[s]
                (qt, b_qt) = qts[s].next()
                (acc, b_acc) = accs[s]
                (yk, b_yk) = yks[s]
                P.op(BQ, lambda e: e.dma_start(out=qt[:, :qw], in_=qT_d[128 * h:128 * h + 128, q0:q0 + qw]),
                     reads=[b_q[h][ti]], writes=[b_qt], dma=True)
                slot_of = {}

                def load_group(g):
                    (kg, vg, b_kg, b_vg) = kvr[s][kvi[s]]
                    kvi[s] = (kvi[s] + 1) % NRING
                    nb = min(4, nkb - 4 * g)
                    tls = range(tile_of(512 * g), tile_of(512 * g + 128 * nb - 1) + 1)
                    P.op(BQ, lambda e: e.dma_start(out=kg[:, :128 * nb],
                                                       in_=kT_d[128 * h:128 * h + 128, 512 * g:512 * g + 128 * nb]),
                         reads=[b_k[h][t] for t in tls], writes=[b_kg], dma=True)
                    P.op(BQ, lambda e: e.dma_start(out=vg[:, :nb, :],
                                                       in_=v_r[:, 4 * g:4 * g + nb, 128 * h:128 * h + 128]),
                         reads=[b for t in tls for tb in b_v[t] for b in tb], writes=[b_vg], dma=True)
                    slot_of[g] = (kg, vg, b_kg, b_vg)

                nload = 0
                while nload < min(3, ngrp):
                    load_group(glist[nload])
                    nload += 1
                st = {}

                def front1(kb):
                    (kg, vg, b_kg, b_vg) = slot_of[kb // 4]
                    (z, b_z) = zb.next()
                    P.op("tensor", lambda e: e.matmul(z[:, :qw], lhsT=kg[:, 128 * (kb % 4):128 * (kb % 4) + 128],
                                                      rhs=qt[:, :qw], start=True, stop=True),
                         reads=[b_qt, b_kg], writes=[b_z])
                    (ee, b_e) = es.next()
                    P.op("scalar", lambda e: e.activation(out=ee[:, :qw], in_=z[:, :qw], func=AF.Exp, scale=QSCALE),
                         reads=[b_z], writes=[b_e])
                    if kb * 128 >= q0:
                        P.op("gpsimd", lambda e: e.affine_select(
                            out=ee[:, :qw], in_=ee[:, :qw], pattern=[[1, qw]], compare_op=ALU.is_gt,
                            fill=0.0, base=q0 - kb * 128, channel_multiplier=-1),
                             reads=[b_e], writes=[b_e])
                    st[kb] = [ee, b_e, None, None]

                def front2(kb):
                    ee, b_e = st[kb][0], st[kb][1]
                    (sp, b_sp) = sps.next()
                    P.op("scalar", lambda e: e.activation(out=sp[:, :qw], in_=ee[:, :qw], func=AF.Ln, bias=1.0),
                         reads=[b_e], writes=[b_sp])
                    st[kb][2] = sp
                    st[kb][3] = b_sp

                front1(kbs[0])
                front2(kbs[0])
                prev = None
                for ii, kb in enumerate(kbs):
                    (ee, b_e, sp, b_sp) = st.pop(kb)
                    if prev is not None:
                        psp, pb_sp = prev[0], prev[1]
                        P.op("tensor", lambda e, psp=psp: e.matmul(
                            acc[:, :qw], lhsT=tri_ex[:], rhs=psp[:, :qw], start=False, stop=False,
                            skip_group_check=True),
                             reads=[pb_sp, b_const], writes=[b_acc])
                    if ii + 1 < n:
                        front1(kbs[ii + 1])
                    P.op("tensor", lambda e, sp=sp, ii=ii: e.matmul(
                        acc[:, :qw], lhsT=tri_in[:], rhs=sp[:, :qw], start=(ii == 0), stop=(ii == n - 1),
                        skip_group_check=True),
                         reads=[b_sp, b_const], writes=[b_acc])
                    if prev is not None:
                        paa, pb_aa, pkb, pii = prev[2], prev[3], prev[4], prev[5]
                        (kg, vg, b_kg, b_vg) = slot_of[pkb // 4]
                        P.op("tensor", lambda e, paa=paa, vg=vg, pkb=pkb, pii=pii: e.matmul(
                            yk[:, :qw], lhsT=vg[:, pkb % 4, :], rhs=paa[:, :qw], start=(pii == 0), stop=False,
                            skip_group_check=True),
                             reads=[pb_aa, b_vg], writes=[b_yk])
                        if pkb % 4 == 0 and nload < ngrp:
                            load_group(glist[nload])
                            nload += 1
                    (tt, b_tt) = tts.next()
                    P.op("scalar", lambda e, tt=tt: e.activation(out=tt[:, :qw], in_=acc[:, :qw], func=AF.Exp,
                                                                 scale=-1.0),
                         reads=[b_acc], writes=[b_tt])
                    if ii + 1 < n:
                        front2(kbs[ii + 1])
                    (aa, b_aa) = aas.next()
                    P.op("vector", lambda e, aa=aa, ee=ee, tt=tt: e.tensor_tensor(
                        out=aa[:, :qw], in0=ee[:, :qw], in1=tt[:, :qw], op=ALU.mult),
                         reads=[b_e, b_tt], writes=[b_aa])
                    prev = (sp, b_sp, aa, b_aa, kb, ii)
                    yield
                paa, pb_aa, pkb, pii = prev[2], prev[3], prev[4], prev[5]
                (kg, vg, b_kg, b_vg) = slot_of[pkb // 4]
                P.op("tensor", lambda e: e.matmul(
                    yk[:, :qw], lhsT=vg[:, pkb % 4, :], rhs=paa[:, :qw], start=(pii == 0), stop=True,
                    skip_group_check=True),
                     reads=[pb_aa, b_vg], writes=[b_yk])
                (sq, b_sq) = ysq.next()
                P.op("scalar", lambda e: e.activation(out=sq[:, :qw], in_=yk[:, :qw], func=AF.Square),
                     reads=[b_yk], writes=[b_sq])
                yield
                (stb_, b_stb_) = zb.next()
                P.op("tensor", lambda e: e.matmul(stb_[:, :qw], lhsT=ones[:], rhs=sq[:, :qw],
                                                  start=True, stop=True, skip_group_check=True),
                     reads=[b_sq, b_const], writes=[b_stb_])
                lnb = lns.next()
                rsb = rss.next()
                rstd_from(P, stb_, b_stb_, lnb, rsb, 1.0 / 128, qw)
                (ost, b_ost) = osts.next()
                P.op("vector", lambda e: e.scalar_tensor_tensor(
                    out=ost[:, :qw], in0=yk[:, :qw], scalar=ppc(l, O_GATTN + h),
                    in1=rsb[0][:, :qw], op0=ALU.mult, op1=ALU.mult),
                     reads=[b_yk, rsb[1], b_pp], writes=[b_ost])
                P.op("gpsimd", lambda e: e.dma_start(
                    out=mxT_d[1024 + 128 * h:1024 + 128 * h + 128, q0:q0 + qw], in_=ost[:, :qw]),
                     reads=[b_ost], writes=[b_mx[8 + h][ti]], dma=True)
                yield

            A.stream = stream
            return A

        for l in range(DEPTH):
            hsrc = h0 if l == 0 else hT
            hsrc_r = hsrc.rearrange("(c p) t -> p c t", p=128)
            hT_r = hT.rearrange("(c p) t -> p c t", p=128)
            last = (l == DEPTH - 1)

            with ExitStack() as ph:
                ht = sb("ht", [128, KC, 512], F32, ph)
                hn = [sb(f"hn{i}", [128, KC, 512], BF16, ph) for i in range(2)]
                slabs = Rot([sb(f"slab{i}", [128, KC, 512], BF16, ph) for i in range(3)], "slab")
                Bst = sb("Bst", [128, 8, 512], F32, ph)
                Cst = sb("Cst", [128, 8, 512], F32, ph)
                cuc = sb("cuc", [128, 8, 2], F32, ph)
                sqs = Rot([sb(f"sq{i}", [128, 512], BF16, ph) for i in range(4)], "sq")
                sqn = Rot([sb(f"sqn{i}", [128, 512], BF16, ph) for i in range(3)], "sqn")
                lns = Rot([sb(f"ln{i}", [128, 512], F32, ph) for i in range(2)], "ln")
                rss = Rot([sb(f"rs{i}", [128, 512], F32, ph) for i in range(2)], "rs")
                cuxs = Rot([sb(f"cux{i}", [128, 514], F32, ph) for i in range(2)], "cux")
                t1s = Rot([sb(f"t1{i}", [128, 512], F32, ph) for i in range(2)], "t1")
                t2s = Rot([sb(f"t2{i}", [128, 512], F32, ph) for i in range(2)], "t2")
                ys = Rot([sb(f"y{i}", [128, 512], F32, ph) for i in range(2)], "y")
                osts = Rot([sb(f"ost{i}", [128, 512], BF16, ph) for i in range(4)], "ost")
                vsts = Rot([sb(f"vst{i}", [128, 512], BF16, ph) for i in range(2)], "vst")
                rs0 = (sb("rs0", [128, 512], F32, ph), Buf("rs0"))
                ln0 = (sb("ln0", [128, 512], F32, ph), Buf("ln0"))
                b_ht = [Buf(f"ht{i}") for i in range(4)]
                b_hn = [Buf("hn0"), Buf("hn1")]
                b_Bst = [Buf() for _ in range(8)]
                b_Cst = [Buf() for _ in range(8)]
                b_cuc = [Buf() for _ in range(8)]
                mbanks = Rot(banks[0:5], "mb")
                sbanks = Rot(banks[5:8], "sbk")

                P.op("vector", lambda e: e.memset(cuc[:], 0.0), writes=b_cuc)

                def norm_stage(src_ap, b_src, sq, b_sq, gcol, dst, dst_r0, b_dst, t0, tw):
                    (stb2, b_stb2) = sbanks.next()
                    P.op("tensor", lambda e: e.matmul(stb2[:, :tw], lhsT=ones[:], rhs=sq[:, :tw],
                                                      start=True, stop=True),
                         reads=[b_sq, b_const], writes=[b_stb2])
                    lnb = lns.next()
                    rsb = rss.next()
                    rstd_from(P, stb2, b_stb2, lnb, rsb, 1.0 / 128, tw)
                    (ost, b_ost) = osts.next()
                    P.op("vector", lambda e: e.scalar_tensor_tensor(
                        out=ost[:, :tw], in0=src_ap[:, :tw], scalar=gcol,
                        in1=rsb[0][:, :tw], op0=ALU.mult, op1=ALU.mult),
                         reads=[b_src, rsb[1], b_pp], writes=[b_ost])
                    P.op("gpsimd", lambda e: e.dma_start(
                        out=dst[dst_r0:dst_r0 + 128, t0:t0 + tw], in_=ost[:, :tw]),
                         reads=[b_ost], writes=[b_dst], dma=True)

                def conv_chunk(g, mb, b_mb, t0, tw, ti, pending):
                    (cux, b_cux) = cuxs.next()
                    (t1, b_t1) = t1s.next()
                    (t2, b_t2) = t2s.next()
                    (y, b_y) = ys.next()
                    (sq, b_sq) = sqs.next()
                    P.op("scalar", lambda e: e.activation(out=cux[:, 0:2], in_=cuc[:, g, :], func=AF.Copy),
                         reads=[b_cuc[g]], writes=[b_cux])
                    P.op("vector", lambda e: e.tensor_tensor(
                        out=cux[:, 2:2 + tw], in0=mb[:, :tw], in1=Cst[:, g, :tw], op=ALU.mult),
                         reads=[b_mb, b_Cst[g]], writes=[b_cux])
                    P.op("scalar", lambda e: e.activation(out=cuc[:, g, :], in_=cux[:, tw:tw + 2], func=AF.Copy),
                         reads=[b_cux], writes=[b_cuc[g]])
                    P.op("vector", lambda e: e.tensor_scalar_mul(
                        out=t1[:, :tw], in0=cux[:, 2:2 + tw], scalar1=ppc(l, O_CW + 16 + g)),
                         reads=[b_cux, b_pp], writes=[b_t1])
                    P.op("vector", lambda e: e.scalar_tensor_tensor(
                        out=t2[:, :tw], in0=cux[:, 1:1 + tw], scalar=ppc(l, O_CW + 8 + g),
                        in1=t1[:, :tw], op0=ALU.mult, op1=ALU.add),
                         reads=[b_cux, b_t1, b_pp], writes=[b_t2])
                    P.op("vector", lambda e: e.scalar_tensor_tensor(
                        out=t1[:, :tw], in0=cux[:, 0:tw], scalar=ppc(l, O_CW + g),
                        in1=t2[:, :tw], op0=ALU.mult, op1=ALU.add),
                         reads=[b_cux, b_t2, b_pp], writes=[b_t1])
                    P.op("vector", lambda e: e.tensor_tensor(
                        out=y[:, :tw], in0=t1[:, :tw], in1=Bst[:, g, :tw], op=ALU.mult),
                         reads=[b_t1, b_Bst[g]], writes=[b_y])
                    P.op("scalar", lambda e: e.activation(out=sq[:, :tw], in_=y[:, :tw], func=AF.Square),
                         reads=[b_y], writes=[b_sq])
                    pending.append((0, lambda: norm_stage(y, b_y, sq, b_sq, ppc(l, O_GCONV + g), mxT_d, 128 * g,
                                                          b_mx[g][ti], t0, tw)))

                def qk_chunk(isq, hh, mb, b_mb, t0, tw, ti, pending):
                    (sq, b_sq) = sqs.next()
                    P.op("scalar", lambda e: e.activation(out=sq[:, :tw], in_=mb[:, :tw], func=AF.Square),
                         reads=[b_mb], writes=[b_sq])
                    dst = qT_d if isq else kT_d
                    bb = b_q if isq else b_k
                    pending.append((0, lambda: norm_stage(mb, b_mb, sq, b_sq, ppc(l, O_GQ if isq else O_GK), dst,
                                                          128 * hh, bb[hh][ti], t0, tw)))

                def mm_chunk(mb, b_mb, slab, b_slab, i, hn_t, bhn, tw):
                    for k in range(KC):
                        P.op("tensor", lambda e, k=k: e.matmul(
                            mb[:, :tw], lhsT=slab[:, k, 128 * i:128 * i + 128], rhs=hn_t[:, k, :tw],
                            start=(k == 0), stop=(k == KC - 1)),
                             reads=[b_slab, bhn], writes=[b_mb])

                def mm_chunk_v(mb, b_mb, slab, b_slab, tb, hn_t, bhn):
                    for k in range(KC):
                        P.op("tensor", lambda e, k=k: e.matmul(
                            mb[:, :], lhsT=hn_t[:, k, 128 * tb:128 * tb + 128], rhs=slab[:, k, :],
                            start=(k == 0), stop=(k == KC - 1)),
                             reads=[b_slab, bhn], writes=[b_mb])

                def load_slab(slab, b_slab, src_ap, deps):
                    P.op("sync", lambda e: e.dma_start(out=slab[:], in_=src_ap),
                         reads=deps, writes=[b_slab], dma=True)

                def normA_tile(ti, t0, tw):
                    hslot = ti % 2
                    hn_t = hn[hslot]
                    bhn = b_hn[hslot]
                    for g4 in range(4):
                        P.op("sync", lambda e, g4=g4: e.dma_start(out=ht[:, 4 * g4:4 * g4 + 4, :tw],
                                                                  in_=hsrc_r[:, 4 * g4:4 * g4 + 4, t0:t0 + tw]),
                             reads=[b_h[ti][g4]], writes=[b_ht[g4]], dma=True)
                    (stb, b_stb) = sbanks.next()
                    for c in range(KC):
                        (sq, b_sq) = sqn.next()
                        P.op("scalar", lambda e, c=c, sq=sq: e.activation(out=sq[:, :tw], in_=ht[:, c, :tw],
                                                                          func=AF.Square),
                             reads=[b_ht[c // 4]], writes=[b_sq])
                        P.op("tensor", lambda e, c=c, sq=sq: e.matmul(
                            stb[:, :tw], lhsT=ones[:], rhs=sq[:, :tw], start=(c == 0), stop=(c == KC - 1)),
                             reads=[b_sq, b_const], writes=[b_stb])
                    rstd_from(P, stb, b_stb, ln0, rs0, 1.0 / D, tw)
                    for c in range(KC):
                        P.op("vector", lambda e, c=c: e.scalar_tensor_tensor(
                            out=hn_t[:, c, :tw], in0=ht[:, c, :tw], scalar=ppc(l, O_GMIX + c),
                            in1=rs0[0][:, :tw], op0=ALU.mult, op1=ALU.mult),
                             reads=[b_ht[c // 4], rs0[1], b_pp], writes=[bhn], nosame=True)

                def phaseA_tile(ti, t0, tw):
                    hslot = ti % 2
                    hn_t = hn[hslot]
                    bhn = b_hn[hslot]
                    pending = []

                    def run_pending(force=False):
                        cur = list(pending)
                        pending[:] = []
                        for (age, fn) in cur:
                            if age >= 1 or force:
                                fn()
                            else:
                                pending.append((age + 1, fn))

                    for j in range(10):
                        (slab, b_slab) = slabs.next()
                        load_slab(slab, b_slab, wb_in[l][j], wdeps(f"in{l}", 0, D))
                        for i in range(4):
                            oc = 4 * j + i
                            (mb, b_mb) = mbanks.next()
                            mm_chunk(mb, b_mb, slab, b_slab, i, hn_t, bhn, tw)
                            run_pending()
                            if oc < 8:
                                P.op("scalar", lambda e, mb=mb, g=oc: e.activation(
                                    out=Bst[:, g, :tw], in_=mb[:, :tw], func=AF.Copy),
                                     reads=[b_mb], writes=[b_Bst[oc]])
                            elif oc < 16:
                                P.op("scalar", lambda e, mb=mb, g=oc - 8: e.activation(
                                    out=Cst[:, g, :tw], in_=mb[:, :tw], func=AF.Copy),
                                     reads=[b_mb], writes=[b_Cst[oc - 8]])
                            elif oc < 24:
                                conv_chunk(oc - 16, mb, b_mb, t0, tw, ti, pending)
                            else:
                                isq = oc < 32
                                qk_chunk(isq, (oc - 24) if isq else (oc - 32), mb, b_mb, t0, tw, ti, pending)
                        if j == 6 and ti + 1 < NT:
                            normA_tile(ti + 1, TILES[ti + 1][0], TILES[ti + 1][1])
                    for j in range(10, 12):
                        (slab, b_slab) = slabs.next()
                        load_slab(slab, b_slab, wb_in[l][j], wdeps(f"in{l}", 0, D))
                        for tb in range(tw // 128):
                            (mb, b_mb) = mbanks.next()
                            mm_chunk_v(mb, b_mb, slab, b_slab, tb, hn_t, bhn)
                            run_pending()
                            (vst, b_vst) = vsts.next()
                            P.op("scalar", lambda e, mb=mb, vst=vst: e.activation(
                                out=vst[:, :], in_=mb[:, :], func=AF.Copy),
                                 reads=[b_mb], writes=[b_vst])
                            P.op("gpsimd", lambda e, vst=vst, tb=tb, j=j: e.dma_start(
                                out=v_d[t0 + 128 * tb:t0 + 128 * tb + 128, 512 * (j - 10):512 * (j - 10) + 512],
                                in_=vst[:, :]),
                                 reads=[b_vst], writes=[b_v[ti][tb][j - 10]], dma=True)
                    run_pending(force=True)
                    run_pending(force=True)

                normA_tile(0, TILES[0][0], TILES[0][1])
                for ti, (t0, tw) in enumerate(TILES):
                    phaseA_tile(ti, t0, tw)
                    if l == 0:
                        emit_conv("rest0", 11)
                if l == 0:
                    emit_conv("rest0")
                P.barrier()
                P.flush()

            with ExitStack() as ph:
                stb_shared = (banks[7], Buf("stb"))
                AT = make_attn(ph, l, [banks[0]], [banks[1], banks[2]], [banks[3], banks[4]], stb_shared)
                ht = sb("cht", [128, KC, 512], F32, ph)
                mx = sb("cmx", [128, KC, 512], BF16, ph)
                hn2 = mx
                hid = sb("chid", [128, 32, 512], BF16, ph)
                slab_t = [sb(f"cslab{i}", [128, 8192], BF16, ph) for i in range(3)]
                slabs = Rot(slab_t, "slab")
                sqs = Rot([sb(f"csq{i}", [128, 512], BF16, ph) for i in range(3)], "sq")
                rls = Rot([sb(f"crl{i}", [128, 512], F32, ph) for i in range(2)], "rl")
                rs0 = (sb("crs0", [128, 512], F32, ph), Buf("rs0"))
                ln0 = (sb("cln0", [128, 512], F32, ph), Buf("ln0"))
                b_ht = [Buf() for _ in range(KC)]
                b_mxs = [Buf() for _ in range(4)]
                b_hid = [Buf() for _ in range(32)]
                mbanks = Rot(banks[5:7], "mb")
                stbk = stb_shared
                mx_r = mxT_d.rearrange("(c p) t -> p c t", p=128)
                o_r = outT.rearrange("(c p) t -> p c t", p=128)

                def load_slab_c(slab, b_slab, ncol, src_ap, deps):
                    P.op("sync", lambda e: e.dma_start(out=slab[:, :], in_=src_ap.rearrange("p k n -> p (k n)")),
                         reads=deps, writes=[b_slab], dma=True)

                def phaseC_gen(ti, t0, tw):
                    for g4 in range(4):
                        P.op("sync", lambda e, g4=g4: e.dma_start(
                            out=ht[:, 4 * g4:4 * g4 + 4, :tw], in_=hsrc_r[:, 4 * g4:4 * g4 + 4, t0:t0 + tw]),
                             reads=[b_h[ti][g4]], writes=b_ht[4 * g4:4 * g4 + 4], dma=True)
                    for g4 in range(4):
                        P.op("sync", lambda e, g4=g4: e.dma_start(
                            out=mx[:, 4 * g4:4 * g4 + 4, :tw], in_=mx_r[:, 4 * g4:4 * g4 + 4, t0:t0 + tw]),
                             reads=[b_mx[c][ti] for c in range(4 * g4, 4 * g4 + 4)], writes=[b_mxs[g4]], dma=True)
                    for j in range(4):
                        (slab, b_slab) = slabs.next()
                        load_slab_c(slab, b_slab, 512, wb_out[l][j], wdeps(f"out{l}", 0, D))
                        for i in range(4):
                            oc = 4 * j + i
                            (mb, b_mb) = mbanks.next()
                            for k in range(KC):
                                P.op("tensor", lambda e, mb=mb, slab=slab, i=i, k=k: e.matmul(
                                    mb[:, :tw], lhsT=slab[:, 512 * k + 128 * i:512 * k + 128 * i + 128],
                                    rhs=mx[:, k, :tw], start=(k == 0), stop=(k == KC - 1), skip_group_check=True),
                                     reads=[b_slab, b_mxs[k // 4]], writes=[b_mb])
                            P.op("vector", lambda e, mb=mb, oc=oc: e.tensor_tensor(
                                out=ht[:, oc, :tw], in0=mb[:, :tw], in1=ht[:, oc, :tw], op=ALU.add),
                                 reads=[b_mb, b_ht[oc]], writes=[b_ht[oc]])
                            (sq, b_sq) = sqs.next()
                            P.op("scalar", lambda e, sq=sq, oc=oc: e.activation(
                                out=sq[:, :tw], in_=ht[:, oc, :tw], func=AF.Square),
                                 reads=[b_ht[oc]], writes=[b_sq])
                            yield
                            P.op("tensor", lambda e, sq=sq, oc=oc: e.matmul(
                                stbk[0][:, :tw], lhsT=ones[:], rhs=sq[:, :tw], start=(oc == 0), stop=(oc == KC - 1),
                                skip_group_check=True),
                                 reads=[b_sq, b_const], writes=[stbk[1]])
                    rstd_from(P, stbk[0], stbk[1], ln0, rs0, 1.0 / D, tw)
                    for c in range(KC):
                        P.op("vector", lambda e, c=c: e.scalar_tensor_tensor(
                            out=hn2[:, c, :tw], in0=ht[:, c, :tw], scalar=ppc(l, O_GMLP + c),
                            in1=rs0[0][:, :tw], op0=ALU.mult, op1=ALU.mult),
                             reads=[b_ht[c], rs0[1], b_pp], writes=[b_mxs[c // 4]], nosame=True)
                    for half in range(2):
                        for j in range(8 * half, 8 * half + 8):
                            (slab, b_slab) = slabs.next()
                            load_slab_c(slab, b_slab, 512, wb_1[l][j], wdeps(f"w1{l}", 0, D))
                            for i in range(4):
                                f = 4 * j + i - 32 * half
                                (mb, b_mb) = mbanks.next()
                                for k in range(KC):
                                    P.op("tensor", lambda e, mb=mb, slab=slab, i=i, k=k: e.matmul(
                                        mb[:, :tw], lhsT=slab[:, 512 * k + 128 * i:512 * k + 128 * i + 128],
                                        rhs=hn2[:, k, :tw], start=(k == 0), stop=(k == KC - 1), skip_group_check=True),
                                         reads=[b_slab, b_mxs[k // 4]], writes=[b_mb])
                                (rl, b_rl) = rls.next()
                                P.op("scalar", lambda e, mb=mb, rl=rl: e.activation(
                                    out=rl[:, :tw], in_=mb[:, :tw], func=AF.Relu),
                                     reads=[b_mb], writes=[b_rl])
                                P.op("vector", lambda e, mb=mb, rl=rl, f=f: e.tensor_tensor(
                                    out=hid[:, f, :tw], in0=mb[:, :tw], in1=rl[:, :tw], op=ALU.mult),
                                     reads=[b_mb, b_rl], writes=[b_hid[f]])
                                yield
                        for j2 in range(8):
                            (slab, b_slab) = slabs.next()
                            load_slab_c(slab, b_slab, 256, wb_2[l][half, j2],
                                        wdeps(f"w2{l}", 4096 * half, 4096 * half + 4096))
                            for i in range(2):
                                oc = 2 * j2 + i
                                (mb, b_mb) = mbanks.next()
                                for k in range(32):
                                    P.op("tensor", lambda e, mb=mb, slab=slab, i=i, k=k: e.matmul(
                                        mb[:, :tw], lhsT=slab[:, 256 * k + 128 * i:256 * k + 128 * i + 128],
                                        rhs=hid[:, k, :tw], start=(k == 0), stop=(k == 31), skip_group_check=True),
                                         reads=[b_slab, b_hid[k]], writes=[b_mb])
                                    if k == 15:
                                        yield
                                P.op("vector", lambda e, mb=mb, oc=oc: e.tensor_tensor(
                                    out=ht[:, oc, :tw], in0=mb[:, :tw], in1=ht[:, oc, :tw], op=ALU.add),
                                     reads=[b_mb, b_ht[oc]], writes=[b_ht[oc]])
                                yield
                            if half == 1 and j2 % 2 == 1:
                                j = j2 // 2
                                if last:
                                    P.op("gpsimd", lambda e, j=j: e.dma_start(
                                        out=o_r[:, 4 * j:4 * j + 4, t0 - 128:t0 - 128 + tw],
                                        in_=ht[:, 4 * j:4 * j + 4, :tw]),
                                         reads=b_ht[4 * j:4 * j + 4], dma=True)
                                else:
                                    P.op("gpsimd", lambda e, j=j: e.dma_start(
                                        out=hT_r[:, 4 * j:4 * j + 4, t0:t0 + tw], in_=ht[:, 4 * j:4 * j + 4, :tw]),
                                         reads=b_ht[4 * j:4 * j + 4], writes=[b_h[ti][j]], dma=True)

                def step(g):
                    try:
                        next(g)
                        return True
                    except StopIteration:
                        return False

                def b_streams(ti):
                    (q0, qw) = TILES[ti]
                    return [(h, ti, q0, qw) for h in range(8)]

                def run_merged(cgen, bspecs):
                    pend = list(bspecs)
                    act = [None, None]
                    c_alive = cgen is not None
                    while c_alive or pend or act[0] is not None or act[1] is not None:
                        for s in range(2):
                            if act[s] is None and pend:
                                (h, ti, q0, qw) = pend.pop(0)
                                act[s] = AT.stream(h, ti, q0, qw, s)
                            if act[s] is not None:
                                if not step(act[s]):
                                    act[s] = None
                        if c_alive:
                            c_alive = step(cgen)

                run_merged(None, b_streams(0))
                for ti, (t0, tw) in enumerate(TILES):
                    cgen = None if (last and ti == 0) else phaseC_gen(ti, t0, tw)
                    bs = b_streams(ti + 1) if ti + 1 < NT else []
                    run_merged(cgen, bs)
                    if l == 0:
                        emit_conv("in1", 2)
                        emit_conv("rest1", 11)
                if l == 0:
                    emit_conv("in1")
                    emit_conv("rest1")
                P.barrier()
                P.flush()
    return nc


_NC_CACHE = {}


def _prep_inputs(inputs, NT=9):
    S = 128 + 512 * (NT - 1)
    nreal = S - 128
    x = np.asarray(inputs["x"], dtype=np.float32)
    meta = np.asarray(inputs["meta_tokens"], dtype=np.float32)
    B = x.shape[0]
    pp = np.zeros((128, DEPTH * NPP), np.float32)
    for l in range(DEPTH):
        o = l * NPP
        pp[:, o + 0:o + 16] = np.asarray(inputs["g_mix"][l]).reshape(16, 128).T
        pp[:, o + 16:o + 32] = np.asarray(inputs["g_mlp"][l]).reshape(16, 128).T
        cw = np.asarray(inputs["conv_w"][l])
        for k in range(3):
            pp[:, o + 32 + 8 * k:o + 32 + 8 * k + 8] = cw[k].reshape(8, 128).T
        pp[:, o + 56:o + 64] = np.asarray(inputs["g_conv_out"][l]).reshape(8, 128).T
        pp[:, o + 64:o + 72] = np.asarray(inputs["g_attn_out"][l]).reshape(8, 128).T
        pp[:, o + 72] = np.asarray(inputs["g_q"][l])
        pp[:, o + 73] = np.asarray(inputs["g_k"][l])
    shared = {
        "pp": pp,
        "w_in": np.ascontiguousarray(inputs["w_in"], dtype=np.float32),
        "w_out": np.ascontiguousarray(inputs["w_out"], dtype=np.float32),
        "w_mlp_in": np.ascontiguousarray(inputs["w_mlp_in"], dtype=np.float32),
        "w_mlp_out": np.ascontiguousarray(inputs["w_mlp_out"], dtype=np.float32),
    }
    in_maps = []
    for b in range(B):
        h0 = np.zeros((D, S), np.float32)
        h0[:, PAD:PAD + NMETA] = meta.T
        h0[:, 128:] = x[b, :nreal].T
        m = dict(shared)
        m["h0"] = h0
        in_maps.append(m)
    return in_maps


def kernel(**inputs):
    NT = 9
    if NT not in _NC_CACHE:
        _NC_CACHE[NT] = build(NT)
    nc = _NC_CACHE[NT]
    in_maps = _prep_inputs(inputs, NT)
    res = run_bass_kernel_spmd(nc, in_maps, core_ids=list(range(8)))
    out = np.stack([np.ascontiguousarray(r["outT"].T) for r in res.results], axis=0)
    return out.astype(np.float32)
```

```python
import numpy as np
import concourse.bass as bass
import concourse.mybir as mybir
from concourse.bass_utils import run_bass_kernel_spmd

F32 = mybir.dt.float32
BF16 = mybir.dt.bfloat16
AF = mybir.ActivationFunctionType
ALU = mybir.AluOpType

D = 2048
KC = 16
DEPTH = 2
PAD = 112
NMETA = 16
SEQ = 4096
DFF = 8192
EPS = 1e-6
NPP = 74
SEQ_STREAMS = False


class Op:
    __slots__ = ("eng", "fn", "waits", "is_dma", "needs_inc", "count", "sem", "semval", "deps")


class Buf:
    __slots__ = ("name", "w", "r")

    def __init__(self, name=""):
        self.name = name
        self.w = None
        self.r = {}


ENGS = ["tensor", "vector", "scalar", "gpsimd", "sync"]


class Prog:
    def __init__(self, nc, esems, dma_pools):
        self.nc = nc
        self.esem = esems
        self.pools = dma_pools
        self.pool_use = {e: [0] * len(p) for e, p in dma_pools.items()}
        self.pool_last = {e: [None] * len(p) for e, p in dma_pools.items()}
        self.pool_idx = {e: 0 for e in dma_pools}
        self.ops = {e: [] for e in ENGS}
        self.count = {e: 0 for e in ENGS}
        self.waited = {e: {} for e in ENGS}
        self.last_compute = {e: None for e in ENGS}
        self.all_dma = []
        self.nops = 0

    def op(self, eng, fn, reads=(), writes=(), dma=False, nosame=False, extra=(), pool=None):
        o = Op()
        o.eng = eng
        o.fn = fn
        o.is_dma = dma
        o.needs_inc = False
        o.count = None
        o.sem = None
        o.semval = None
        deps = []
        for b in reads:
            if b.w is not None:
                deps.append(b.w)
        for b in writes:
            if b.w is not None:
                deps.append(b.w)
            deps.extend(b.r.values())
        deps.extend(extra)
        fdeps = []
        seen = set()
        for d in deps:
            if id(d) in seen:
                continue
            seen.add(id(d))
            if (not d.is_dma) and d.eng == eng:
                if dma:
                    pass
                elif eng == "tensor" or nosame:
                    continue
            fdeps.append(d)
        o.deps = fdeps
        if dma:
            pk = pool or eng
            pl = self.pools[pk]
            i = self.pool_idx[pk]
            self.pool_idx[pk] = (i + 1) % len(pl)
            prev = self.pool_last[pk][i]
            if prev is not None:
                o.deps.append(prev)
            self.pool_use[pk][i] += 1
            o.sem = pl[i]
            o.semval = 16 * self.pool_use[pk][i]
            self.pool_last[pk][i] = o
            self.all_dma.append(o)
        for d in o.deps:
            if not d.is_dma:
                d.needs_inc = True
        for b in reads:
            key = ("dma", id(o)) if dma else eng
            b.r[key] = o
        for b in writes:
            b.w = o
            b.r = {}
        self.ops[eng].append(o)
        if not dma and fn is not None:
            self.last_compute[eng] = o
        self.nops += 1
        return o

    def barrier(self):
        lasts = [self.last_compute[e] for e in ENGS if self.last_compute[e] is not None]
        dm = list(self.all_dma)
        for e in ENGS:
            self.op(e, None, extra=dm + lasts)
        self.all_dma = []

    def flush(self):
        nc = self.nc
        for e in ENGS:
            c = self.count[e]
            for o in self.ops[e]:
                if not o.is_dma and o.needs_inc:
                    c += 1
                    o.count = c
            self.count[e] = c
        prog = self

        def emit(e_name, eng):
            waited = prog.waited[e_name]
            for o in prog.ops[e_name]:
                for d in o.deps:
                    if d.is_dma:
                        sem, val = d.sem, d.semval
                    else:
                        sem, val = prog.esem[d.eng], d.count
                    key = id(sem)
                    if waited.get(key, 0) < val:
                        eng.wait_ge(sem, val)
                        waited[key] = val
                if o.fn is None:
                    continue
                ins = o.fn(eng)
                if o.is_dma:
                    ins.then_inc(o.sem, 16)
                elif o.needs_inc:
                    ins.then_inc(prog.esem[e_name], 1)

        with nc.Block() as block:
            @block.tensor
            def _(t):
                emit("tensor", t)

            @block.vector
            def _(v):
                emit("vector", v)

            @block.scalar
            def _(s):
                emit("scalar", s)

            @block.gpsimd
            def _(g):
                emit("gpsimd", g)

            @block.sync
            def _(s):
                emit("sync", s)
        self.ops = {e: [] for e in ENGS}


class Rot:
    def __init__(self, aps, name=""):
        self.items = [(a, Buf(name + str(i))) for i, a in enumerate(aps)]
        self.i = 0

    def next(self):
        it = self.items[self.i]
        self.i = (self.i + 1) % len(self.items)
        return it


def build(NT=9):
    S = 128 + 512 * (NT - 1)
    NREAL = S - 128
    TILES = [(0, 128)] + [(128 + 512 * i, 512) for i in range(NT - 1)]
    NKB = S // 128
    QSCALE = float(128 ** -0.5)

    nc = bass.Bass("TRN2", target_bir_lowering=False)
    h0 = nc.dram_tensor("h0", [D, S], F32, kind="ExternalInput").ap()
    pp_d = nc.dram_tensor("pp", [128, DEPTH * NPP], F32, kind="ExternalInput").ap()
    w_in = nc.dram_tensor("w_in", [DEPTH, D, 3 * D], F32, kind="ExternalInput").ap()
    w_out = nc.dram_tensor("w_out", [DEPTH, D, D], F32, kind="ExternalInput").ap()
    w_1 = nc.dram_tensor("w_mlp_in", [DEPTH, D, DFF], F32, kind="ExternalInput").ap()
    w_2 = nc.dram_tensor("w_mlp_out", [DEPTH, DFF, D], F32, kind="ExternalInput").ap()
    outT = nc.dram_tensor("outT", [D, NREAL], F32, kind="ExternalOutput").ap()

    hT = nc.dram_tensor("hT", [D, S], F32).ap()
    wb_in = [nc.dram_tensor(f"wb_in{l}", [12, 128, KC, 512], BF16).ap() for l in range(DEPTH)]
    wb_out = [nc.dram_tensor(f"wb_out{l}", [4, 128, KC, 512], BF16).ap() for l in range(DEPTH)]
    wb_1 = [nc.dram_tensor(f"wb_1{l}", [16, 128, KC, 512], BF16).ap() for l in range(DEPTH)]
    wb_2 = [nc.dram_tensor(f"wb_2{l}", [2, 8, 128, 32, 256], BF16).ap() for l in range(DEPTH)]
    qT_d = nc.dram_tensor("qT", [1024, S], BF16).ap()
    kT_d = nc.dram_tensor("kT", [1024, S], BF16).ap()
    v_d = nc.dram_tensor("v", [S, 1024], BF16).ap()
    mxT_d = nc.dram_tensor("mxT", [D, S], BF16).ap()

    from contextlib import ExitStack
    with ExitStack() as top:
        _cnt = [0]

        def sb(name, shape, dt, stack=top):
            _cnt[0] += 1
            return stack.enter_context(nc.sbuf_tensor(f"{name}_{_cnt[0]}", shape, dt))

        def ps(name, stack=top):
            return stack.enter_context(nc.psum_tensor(name, [128, 512], F32))

        esems = {e: top.enter_context(nc.semaphore("es_" + e)) for e in ENGS}
        pools = {
            "sync": [top.enter_context(nc.semaphore(f"ds{i}")) for i in range(12)],
            "gpsimd": [top.enter_context(nc.semaphore(f"dg{i}")) for i in range(12)],
            "gconv": [top.enter_context(nc.semaphore(f"dc{i}")) for i in range(8)],
        }
        P = Prog(nc, esems, pools)

        pp = sb("pp_sb", [128, DEPTH * NPP], F32)
        ones = sb("ones", [128, 128], BF16)
        tri_in = sb("tri_in", [128, 128], BF16)
        tri_ex = sb("tri_ex", [128, 128], BF16)
        onesf = sb("onesf", [128, 128], F32)
        banks = [ps(f"bank{i}") for i in range(8)]

        b_pp = Buf("pp")
        b_const = Buf("const")
        P.op("sync", lambda e: e.dma_start(out=pp[:], in_=pp_d[:, :]), writes=[b_pp], dma=True)
        P.op("vector", lambda e: e.memset(onesf[:], 1.0), writes=[b_const])
        P.op("vector", lambda e: e.tensor_copy(out=ones[:], in_=onesf[:]), reads=[b_const], writes=[b_const])
        P.op("gpsimd", lambda e: e.affine_select(out=tri_in[:], in_=ones[:], pattern=[[-1, 128]],
                                                 compare_op=ALU.is_ge, fill=0.0, base=0, channel_multiplier=1),
             reads=[b_const], writes=[b_const])
        P.op("gpsimd", lambda e: e.affine_select(out=tri_ex[:], in_=ones[:], pattern=[[1, 128]],
                                                 compare_op=ALU.is_gt, fill=0.0, base=0, channel_multiplier=-1),
             reads=[b_const], writes=[b_const])

        wbufs = {}

        conv_q = {}

        def convert(grp, name, src, dstfn, rows, ncol):
            bl = []
            for c in range(rows // 128):
                b = Buf(f"{name}{c}")

                def rec(c=c, b=b):
                    s_ap = src[c * 128:(c + 1) * 128, :].rearrange("p (j n) -> p j n", n=ncol)
                    d_ap = dstfn(c).rearrange("j p n -> p j n")
                    P.op("gpsimd", lambda e: e.dma_start(out=d_ap, in_=s_ap),
                         writes=[b], dma=True, pool="gconv")
                conv_q.setdefault(grp, []).append(rec)
                bl.append(b)
            wbufs[name] = (bl, 128)

        def emit_conv(grp, n=None):
            q = conv_q.get(grp, [])
            k = len(q) if n is None else min(n, len(q))
            for _ in range(k):
                q.pop(0)()

        in0_bufs = []
        for j in range(12):
            b = Buf(f"in0s{j}")

            def rec(j=j, b=b):
                s_ap = w_in[0][:, 512 * j:512 * j + 512].rearrange("(c p) n -> p c n", p=128)
                P.op("gpsimd", lambda e: e.dma_start(out=wb_in[0][j], in_=s_ap),
                     writes=[b], dma=True, pool="gconv")
            conv_q.setdefault("in0", []).append(rec)
            in0_bufs.append(b)
        convert("in1", "in1", w_in[1], lambda c: wb_in[1][:, :, c, :], D, 512)
        for l in range(DEPTH):
            convert(f"rest{l}", f"out{l}", w_out[l], lambda c, l=l: wb_out[l][:, :, c, :], D, 512)
            convert(f"rest{l}", f"w1{l}", w_1[l], lambda c, l=l: wb_1[l][:, :, c, :], D, 512)
            convert(f"rest{l}", f"w2{l}", w_2[l], lambda c, l=l: wb_2[l][c // 32, :, :, c % 32, :], DFF, 256)
        emit_conv("in0")

        def wdeps(name, r0, r1):
            bl, rb = wbufs[name]
            return [bl[i] for i in range(r0 // rb, (r1 + rb - 1) // rb)]

        b_h = [[Buf() for _ in range(4)] for t in range(NT)]
        b_q = [[Buf() for _ in range(NT)] for _ in range(8)]
        b_k = [[Buf() for _ in range(NT)] for _ in range(8)]
        b_v = [[[Buf(), Buf()] for _ in range(4)] for t in range(NT)]
        b_mx = [[Buf() for _ in range(NT)] for _ in range(16)]
        out_ops = []

        def ppc(l, off, n=1):
            c = l * NPP + off
            return pp[:, c:c + n]

        O_GMIX, O_GMLP, O_CW, O_GCONV, O_GATTN, O_GQ, O_GK = 0, 16, 32, 56, 64, 72, 73

        def rstd_from(P_, bank_ap, b_bank, lnbuf, rs, scale, tw):
            (ln_ap, b_ln) = lnbuf
            (rs_ap, b_rs) = rs
            P_.op("scalar", lambda e: e.activation(out=ln_ap[:, :tw], in_=bank_ap[:, :tw], func=AF.Ln,
                                                   bias=EPS, scale=scale),
                  reads=[b_bank], writes=[b_ln])
            P_.op("scalar", lambda e: e.activation(out=rs_ap[:, :tw], in_=ln_ap[:, :tw], func=AF.Exp,
                                                   scale=-0.5),
                  reads=[b_ln], writes=[b_rs])

        class _NS:
            pass

        def tile_of(p):
            return 0 if p < 128 else 1 + (p - 128) // 512

        def make_attn(ph, l, zbanks, accbanks, ybanks, stbk):
            A = _NS()
            zb = Rot(zbanks, "z")
            es_ = [Rot([sb("e", [128, 512], F32, ph) for i in range(3)], "e") for s in range(2)]
            sps_ = [Rot([sb("sp", [128, 512], BF16, ph) for i in range(3)], "sp") for s in range(2)]
            tts_ = [Rot([sb("tt", [128, 512], F32, ph) for i in range(2)], "tt") for s in range(2)]
            aas_ = [Rot([sb("aa", [128, 512], BF16, ph) for i in range(3)], "aa") for s in range(2)]
            ysq = Rot([sb("ysq", [128, 512], BF16, ph) for i in range(2)], "ysq")
            lns = Rot([sb("bln", [128, 512], F32, ph) for i in range(2)], "ln")
            rss = Rot([sb("brs", [128, 512], F32, ph) for i in range(2)], "rs")
            osts = Rot([sb("bost", [128, 512], BF16, ph) for i in range(2)], "ost")
            qts = [Rot([sb("qt", [128, 512], BF16, ph) for i in range(2)], "qt") for s in range(2)]
            kvr = [[(sb("kg", [128, 512], BF16, ph), sb("vg", [128, 4, 128], BF16, ph), Buf(), Buf())
                    for i in range(3)] for s in range(2)]
            kvi = [0, 0]
            accs = [(accbanks[s], Buf()) for s in range(2)]
            yks = [(ybanks[s], Buf()) for s in range(2)]
            v_r = v_d.rearrange("(b p) d -> p b d", p=128)

            def stream(h, ti, q0, qw, s):
                nkb = (q0 + qw) // 128
                kbs = list(range(nkb - 1, -1, -1))
                n = len(kbs)
                ngrp = (nkb + 3) // 4
                glist = list(range(ngrp - 1, -1, -1))
                es, sps, tts, aas = es_[s], sps_[s], tts_[s], aas_[s]
                (qt, b_qt) = qts[s].next()
                (acc, b_acc) = accs[s]
                (yk, b_yk) = yks[s]
                P.op("sync", lambda e: e.dma_start(out=qt[:, :qw], in_=qT_d[128 * h:128 * h + 128, q0:q0 + qw]),
                     reads=[b_q[h][ti]], writes=[b_qt], dma=True)
                slot_of = {}

                def load_group(g):
                    (kg, vg, b_kg, b_vg) = kvr[s][kvi[s]]
                    kvi[s] = (kvi[s] + 1) % 3
                    nb = min(4, nkb - 4 * g)
                    tls = range(tile_of(512 * g), tile_of(512 * g + 128 * nb - 1) + 1)
                    P.op("sync", lambda e: e.dma_start(out=kg[:, :128 * nb],
                                                       in_=kT_d[128 * h:128 * h + 128, 512 * g:512 * g + 128 * nb]),
                         reads=[b_k[h][t] for t in tls], writes=[b_kg], dma=True)
                    P.op("sync", lambda e: e.dma_start(out=vg[:, :nb, :],
                                                       in_=v_r[:, 4 * g:4 * g + nb, 128 * h:128 * h + 128]),
                         reads=[b for t in tls for tb in b_v[t] for b in tb], writes=[b_vg], dma=True)
                    slot_of[g] = (kg, vg, b_kg, b_vg)

                nload = 0
                while nload < min(3, ngrp):
                    load_group(glist[nload])
                    nload += 1
                st = {}

                def front1(kb):
                    (kg, vg, b_kg, b_vg) = slot_of[kb // 4]
                    (z, b_z) = zb.next()
                    P.op("tensor", lambda e: e.matmul(z[:, :qw], lhsT=kg[:, 128 * (kb % 4):128 * (kb % 4) + 128],
                                                      rhs=qt[:, :qw], start=True, stop=True),
                         reads=[b_qt, b_kg], writes=[b_z])
                    (ee, b_e) = es.next()
                    P.op("scalar", lambda e: e.activation(out=ee[:, :qw], in_=z[:, :qw], func=AF.Exp, scale=QSCALE),
                         reads=[b_z], writes=[b_e])
                    if kb * 128 >= q0:
                        P.op("gpsimd", lambda e: e.affine_select(
                            out=ee[:, :qw], in_=ee[:, :qw], pattern=[[1, qw]], compare_op=ALU.is_gt,
                            fill=0.0, base=q0 - kb * 128, channel_multiplier=-1),
                             reads=[b_e], writes=[b_e])
                    st[kb] = [ee, b_e, None, None]

                def front2(kb):
                    ee, b_e = st[kb][0], st[kb][1]
                    (sp, b_sp) = sps.next()
                    P.op("scalar", lambda e: e.activation(out=sp[:, :qw], in_=ee[:, :qw], func=AF.Ln, bias=1.0),
                         reads=[b_e], writes=[b_sp])
                    st[kb][2] = sp
                    st[kb][3] = b_sp

                front1(kbs[0])
                front2(kbs[0])
                prev = None
                for ii, kb in enumerate(kbs):
                    (ee, b_e, sp, b_sp) = st.pop(kb)
                    if prev is not None:
                        psp, pb_sp = prev[0], prev[1]
                        P.op("tensor", lambda e, psp=psp: e.matmul(
                            acc[:, :qw], lhsT=tri_ex[:], rhs=psp[:, :qw], start=False, stop=False,
                            skip_group_check=True),
                             reads=[pb_sp, b_const], writes=[b_acc])
                    if ii + 1 < n:
                        front1(kbs[ii + 1])
                    P.op("tensor", lambda e, sp=sp, ii=ii: e.matmul(
                        acc[:, :qw], lhsT=tri_in[:], rhs=sp[:, :qw], start=(ii == 0), stop=(ii == n - 1),
                        skip_group_check=True),
                         reads=[b_sp, b_const], writes=[b_acc])
                    if prev is not None:
                        paa, pb_aa, pkb, pii = prev[2], prev[3], prev[4], prev[5]
                        (kg, vg, b_kg, b_vg) = slot_of[pkb // 4]
                        P.op("tensor", lambda e, paa=paa, vg=vg, pkb=pkb, pii=pii: e.matmul(
                            yk[:, :qw], lhsT=vg[:, pkb % 4, :], rhs=paa[:, :qw], start=(pii == 0), stop=False,
                            skip_group_check=True),
                             reads=[pb_aa, b_vg], writes=[b_yk])
                        if pkb % 4 == 0 and nload < ngrp:
                            load_group(glist[nload])
                            nload += 1
                    (tt, b_tt) = tts.next()
                    P.op("scalar", lambda e, tt=tt: e.activation(out=tt[:, :qw], in_=acc[:, :qw], func=AF.Exp,
                                                                 scale=-1.0),
                         reads=[b_acc], writes=[b_tt])
                    if ii + 1 < n:
                        front2(kbs[ii + 1])
                    (aa, b_aa) = aas.next()
                    P.op("vector", lambda e, aa=aa, ee=ee, tt=tt: e.tensor_tensor(
                        out=aa[:, :qw], in0=ee[:, :qw], in1=tt[:, :qw], op=ALU.mult),
                         reads=[b_e, b_tt], writes=[b_aa])
                    prev = (sp, b_sp, aa, b_aa, kb, ii)
                    yield
                paa, pb_aa, pkb, pii = prev[2], prev[3], prev[4], prev[5]
                (kg, vg, b_kg, b_vg) = slot_of[pkb // 4]
                P.op("tensor", lambda e: e.matmul(
                    yk[:, :qw], lhsT=vg[:, pkb % 4, :], rhs=paa[:, :qw], start=(pii == 0), stop=True,
                    skip_group_check=True),
                     reads=[pb_aa, b_vg], writes=[b_yk])
                (sq, b_sq) = ysq.next()
                P.op("scalar", lambda e: e.activation(out=sq[:, :qw], in_=yk[:, :qw], func=AF.Square),
                     reads=[b_yk], writes=[b_sq])
                yield
                (stb_, b_stb_) = zb.next()
                P.op("tensor", lambda e: e.matmul(stb_[:, :qw], lhsT=ones[:], rhs=sq[:, :qw],
                                                  start=True, stop=True, skip_group_check=True),
                     reads=[b_sq, b_const], writes=[b_stb_])
                lnb = lns.next()
                rsb = rss.next()
                rstd_from(P, stb_, b_stb_, lnb, rsb, 1.0 / 128, qw)
                (ost, b_ost) = osts.next()
                P.op("vector", lambda e: e.scalar_tensor_tensor(
                    out=ost[:, :qw], in0=yk[:, :qw], scalar=ppc(l, O_GATTN + h),
                    in1=rsb[0][:, :qw], op0=ALU.mult, op1=ALU.mult),
                     reads=[b_yk, rsb[1], b_pp], writes=[b_ost])
                P.op("gpsimd", lambda e: e.dma_start(
                    out=mxT_d[1024 + 128 * h:1024 + 128 * h + 128, q0:q0 + qw], in_=ost[:, :qw]),
                     reads=[b_ost], writes=[b_mx[8 + h][ti]], dma=True)
                yield

            A.stream = stream
            return A

        for l in range(DEPTH):
            hsrc = h0 if l == 0 else hT
            hsrc_r = hsrc.rearrange("(c p) t -> p c t", p=128)
            hT_r = hT.rearrange("(c p) t -> p c t", p=128)
            last = (l == DEPTH - 1)

            with ExitStack() as ph:
                ht = sb("ht", [128, KC, 512], F32, ph)
                hn = [sb(f"hn{i}", [128, KC, 512], BF16, ph) for i in range(2)]
                slabs = Rot([sb(f"slab{i}", [128, KC, 512], BF16, ph) for i in range(3)], "slab")
                Bst = sb("Bst", [128, 8, 512], F32, ph)
                Cst = sb("Cst", [128, 8, 512], F32, ph)
                cuc = sb("cuc", [128, 8, 2], F32, ph)
                sqs = Rot([sb(f"sq{i}", [128, 512], BF16, ph) for i in range(4)], "sq")
                sqn = Rot([sb(f"sqn{i}", [128, 512], BF16, ph) for i in range(3)], "sqn")
                lns = Rot([sb(f"ln{i}", [128, 512], F32, ph) for i in range(2)], "ln")
                rss = Rot([sb(f"rs{i}", [128, 512], F32, ph) for i in range(2)], "rs")
                cuxs = Rot([sb(f"cux{i}", [128, 514], F32, ph) for i in range(2)], "cux")
                t1s = Rot([sb(f"t1{i}", [128, 512], F32, ph) for i in range(2)], "t1")
                t2s = Rot([sb(f"t2{i}", [128, 512], F32, ph) for i in range(2)], "t2")
                ys = Rot([sb(f"y{i}", [128, 512], F32, ph) for i in range(2)], "y")
                osts = Rot([sb(f"ost{i}", [128, 512], BF16, ph) for i in range(4)], "ost")
                vsts = Rot([sb(f"vst{i}", [128, 512], BF16, ph) for i in range(2)], "vst")
                rs0 = (sb("rs0", [128, 512], F32, ph), Buf("rs0"))
                ln0 = (sb("ln0", [128, 512], F32, ph), Buf("ln0"))
                b_ht = [Buf(f"ht{i}") for i in range(4)]
                b_hn = [Buf("hn0"), Buf("hn1")]
                b_Bst = [Buf() for _ in range(8)]
                b_Cst = [Buf() for _ in range(8)]
                b_cuc = [Buf() for _ in range(8)]
                mbanks = Rot(banks[0:5], "mb")
                sbanks = Rot(banks[5:8], "sbk")

                P.op("vector", lambda e: e.memset(cuc[:], 0.0), writes=b_cuc)

                def norm_stage(src_ap, b_src, sq, b_sq, gcol, dst, dst_r0, b_dst, t0, tw):
                    (stb2, b_stb2) = sbanks.next()
                    P.op("tensor", lambda e: e.matmul(stb2[:, :tw], lhsT=ones[:], rhs=sq[:, :tw],
                                                      start=True, stop=True),
                         reads=[b_sq, b_const], writes=[b_stb2])
                    lnb = lns.next()
                    rsb = rss.next()
                    rstd_from(P, stb2, b_stb2, lnb, rsb, 1.0 / 128, tw)
                    (ost, b_ost) = osts.next()
                    P.op("vector", lambda e: e.scalar_tensor_tensor(
                        out=ost[:, :tw], in0=src_ap[:, :tw], scalar=gcol,
                        in1=rsb[0][:, :tw], op0=ALU.mult, op1=ALU.mult),
                         reads=[b_src, rsb[1], b_pp], writes=[b_ost])
                    P.op("gpsimd", lambda e: e.dma_start(
                        out=dst[dst_r0:dst_r0 + 128, t0:t0 + tw], in_=ost[:, :tw]),
                         reads=[b_ost], writes=[b_dst], dma=True)

                def conv_chunk(g, mb, b_mb, t0, tw, ti, pending):
                    (cux, b_cux) = cuxs.next()
                    (t1, b_t1) = t1s.next()
                    (t2, b_t2) = t2s.next()
                    (y, b_y) = ys.next()
                    (sq, b_sq) = sqs.next()
                    P.op("scalar", lambda e: e.activation(out=cux[:, 0:2], in_=cuc[:, g, :], func=AF.Copy),
                         reads=[b_cuc[g]], writes=[b_cux])
                    P.op("vector", lambda e: e.tensor_tensor(
                        out=cux[:, 2:2 + tw], in0=mb[:, :tw], in1=Cst[:, g, :tw], op=ALU.mult),
                         reads=[b_mb, b_Cst[g]], writes=[b_cux])
                    P.op("scalar", lambda e: e.activation(out=cuc[:, g, :], in_=cux[:, tw:tw + 2], func=AF.Copy),
                         reads=[b_cux], writes=[b_cuc[g]])
                    P.op("vector", lambda e: e.tensor_scalar_mul(
                        out=t1[:, :tw], in0=cux[:, 2:2 + tw], scalar1=ppc(l, O_CW + 16 + g)),
                         reads=[b_cux, b_pp], writes=[b_t1])
                    P.op("vector", lambda e: e.scalar_tensor_tensor(
                        out=t2[:, :tw], in0=cux[:, 1:1 + tw], scalar=ppc(l, O_CW + 8 + g),
                        in1=t1[:, :tw], op0=ALU.mult, op1=ALU.add),
                         reads=[b_cux, b_t1, b_pp], writes=[b_t2])
                    P.op("vector", lambda e: e.scalar_tensor_tensor(
                        out=t1[:, :tw], in0=cux[:, 0:tw], scalar=ppc(l, O_CW + g),
                        in1=t2[:, :tw], op0=ALU.mult, op1=ALU.add),
                         reads=[b_cux, b_t2, b_pp], writes=[b_t1])
                    P.op("vector", lambda e: e.tensor_tensor(
                        out=y[:, :tw], in0=t1[:, :tw], in1=Bst[:, g, :tw], op=ALU.mult),
                         reads=[b_t1, b_Bst[g]], writes=[b_y])
                    P.op("scalar", lambda e: e.activation(out=sq[:, :tw], in_=y[:, :tw], func=AF.Square),
                         reads=[b_y], writes=[b_sq])
                    pending.append((0, lambda: norm_stage(y, b_y, sq, b_sq, ppc(l, O_GCONV + g), mxT_d, 128 * g,
                                                          b_mx[g][ti], t0, tw)))

                def qk_chunk(isq, hh, mb, b_mb, t0, tw, ti, pending):
                    (sq, b_sq) = sqs.next()
                    P.op("scalar", lambda e: e.activation(out=sq[:, :tw], in_=mb[:, :tw], func=AF.Square),
                         reads=[b_mb], writes=[b_sq])
                    dst = qT_d if isq else kT_d
                    bb = b_q if isq else b_k
                    pending.append((0, lambda: norm_stage(mb, b_mb, sq, b_sq, ppc(l, O_GQ if isq else O_GK), dst,
                                                          128 * hh, bb[hh][ti], t0, tw)))

                def mm_chunk(mb, b_mb, slab, b_slab, i, hn_t, bhn, tw):
                    for k in range(KC):
                        P.op("tensor", lambda e, k=k: e.matmul(
                            mb[:, :tw], lhsT=slab[:, k, 128 * i:128 * i + 128], rhs=hn_t[:, k, :tw],
                            start=(k == 0), stop=(k == KC - 1)),
                             reads=[b_slab, bhn], writes=[b_mb])

                def mm_chunk_v(mb, b_mb, slab, b_slab, tb, hn_t, bhn):
                    for k in range(KC):
                        P.op("tensor", lambda e, k=k: e.matmul(
                            mb[:, :], lhsT=hn_t[:, k, 128 * tb:128 * tb + 128], rhs=slab[:, k, :],
                            start=(k == 0), stop=(k == KC - 1)),
                             reads=[b_slab, bhn], writes=[b_mb])

                def load_slab(slab, b_slab, src_ap, deps):
                    P.op("sync", lambda e: e.dma_start(out=slab[:], in_=src_ap),
                         reads=deps, writes=[b_slab], dma=True)

                def normA_tile(ti, t0, tw):
                    hslot = ti % 2
                    hn_t = hn[hslot]
                    bhn = b_hn[hslot]
                    for g4 in range(4):
                        P.op("sync", lambda e, g4=g4: e.dma_start(out=ht[:, 4 * g4:4 * g4 + 4, :tw],
                                                                  in_=hsrc_r[:, 4 * g4:4 * g4 + 4, t0:t0 + tw]),
                             reads=[b_h[ti][g4]], writes=[b_ht[g4]], dma=True)
                    (stb, b_stb) = sbanks.next()
                    for c in range(KC):
                        (sq, b_sq) = sqn.next()
                        P.op("scalar", lambda e, c=c, sq=sq: e.activation(out=sq[:, :tw], in_=ht[:, c, :tw],
                                                                          func=AF.Square),
                             reads=[b_ht[c // 4]], writes=[b_sq])
                        P.op("tensor", lambda e, c=c, sq=sq: e.matmul(
                            stb[:, :tw], lhsT=ones[:], rhs=sq[:, :tw], start=(c == 0), stop=(c == KC - 1)),
                             reads=[b_sq, b_const], writes=[b_stb])
                    rstd_from(P, stb, b_stb, ln0, rs0, 1.0 / D, tw)
                    for c in range(KC):
                        P.op("vector", lambda e, c=c: e.scalar_tensor_tensor(
                            out=hn_t[:, c, :tw], in0=ht[:, c, :tw], scalar=ppc(l, O_GMIX + c),
                            in1=rs0[0][:, :tw], op0=ALU.mult, op1=ALU.mult),
                             reads=[b_ht[c // 4], rs0[1], b_pp], writes=[bhn], nosame=True)

                def phaseA_tile(ti, t0, tw):
                    hslot = ti % 2
                    hn_t = hn[hslot]
                    bhn = b_hn[hslot]
                    pending = []

                    def run_pending(force=False):
                        cur = list(pending)
                        pending[:] = []
                        for (age, fn) in cur:
                            if age >= 1 or force:
                                fn()
                            else:
                                pending.append((age + 1, fn))

                    for j in range(10):
                        (slab, b_slab) = slabs.next()
                        load_slab(slab, b_slab, wb_in[l][j], [in0_bufs[j]] if l == 0 else wdeps(f"in{l}", 0, D))
                        for i in range(4):
                            oc = 4 * j + i
                            (mb, b_mb) = mbanks.next()
                            mm_chunk(mb, b_mb, slab, b_slab, i, hn_t, bhn, tw)
                            run_pending()
                            if oc < 8:
                                P.op("scalar", lambda e, mb=mb, g=oc: e.activation(
                                    out=Bst[:, g, :tw], in_=mb[:, :tw], func=AF.Copy),
                                     reads=[b_mb], writes=[b_Bst[oc]])
                            elif oc < 16:
                                P.op("scalar", lambda e, mb=mb, g=oc - 8: e.activation(
                                    out=Cst[:, g, :tw], in_=mb[:, :tw], func=AF.Copy),
                                     reads=[b_mb], writes=[b_Cst[oc - 8]])
                            elif oc < 24:
                                conv_chunk(oc - 16, mb, b_mb, t0, tw, ti, pending)
                            else:
                                isq = oc < 32
                                qk_chunk(isq, (oc - 24) if isq else (oc - 32), mb, b_mb, t0, tw, ti, pending)
                        if j == 6 and ti + 1 < NT:
                            normA_tile(ti + 1, TILES[ti + 1][0], TILES[ti + 1][1])
                    for j in range(10, 12):
                        (slab, b_slab) = slabs.next()
                        load_slab(slab, b_slab, wb_in[l][j], [in0_bufs[j]] if l == 0 else wdeps(f"in{l}", 0, D))
                        for tb in range(tw // 128):
                            (mb, b_mb) = mbanks.next()
                            mm_chunk_v(mb, b_mb, slab, b_slab, tb, hn_t, bhn)
                            run_pending()
                            (vst, b_vst) = vsts.next()
                            P.op("scalar", lambda e, mb=mb, vst=vst: e.activation(
                                out=vst[:, :], in_=mb[:, :], func=AF.Copy),
                                 reads=[b_mb], writes=[b_vst])
                            P.op("gpsimd", lambda e, vst=vst, tb=tb, j=j: e.dma_start(
                                out=v_d[t0 + 128 * tb:t0 + 128 * tb + 128, 512 * (j - 10):512 * (j - 10) + 512],
                                in_=vst[:, :]),
                                 reads=[b_vst], writes=[b_v[ti][tb][j - 10]], dma=True)
                    run_pending(force=True)
                    run_pending(force=True)

                normA_tile(0, TILES[0][0], TILES[0][1])
                for ti, (t0, tw) in enumerate(TILES):
                    phaseA_tile(ti, t0, tw)
                    if l == 0:
                        emit_conv("rest0", 11)
                if l == 0:
                    emit_conv("rest0")
                P.barrier()
                P.flush()

            with ExitStack() as ph:
                stb_shared = (banks[7], Buf("stb"))
                AT = make_attn(ph, l, [banks[0]], [banks[1], banks[2]], [banks[3], banks[4]], stb_shared)
                ht = sb("cht", [128, KC, 512], F32, ph)
                mx = sb("cmx", [128, KC, 512], BF16, ph)
                hn2 = mx
                hid = sb("chid", [128, 32, 512], BF16, ph)
                slab_t = [sb(f"cslab{i}", [128, 8192], BF16, ph) for i in range(3)]
                slabs = Rot(slab_t, "slab")
                sqs = Rot([sb(f"csq{i}", [128, 512], BF16, ph) for i in range(3)], "sq")
                rls = Rot([sb(f"crl{i}", [128, 512], F32, ph) for i in range(2)], "rl")
                rs0 = (sb("crs0", [128, 512], F32, ph), Buf("rs0"))
                ln0 = (sb("cln0", [128, 512], F32, ph), Buf("ln0"))
                b_ht = [Buf() for _ in range(KC)]
                b_mxs = [Buf() for _ in range(4)]
                b_hid = [Buf() for _ in range(32)]
                mbanks = Rot(banks[5:7], "mb")
                stbk = stb_shared
                mx_r = mxT_d.rearrange("(c p) t -> p c t", p=128)
                o_r = outT.rearrange("(c p) t -> p c t", p=128)

                def load_slab_c(slab, b_slab, ncol, src_ap, deps):
                    P.op("sync", lambda e: e.dma_start(out=slab[:, :], in_=src_ap.rearrange("p k n -> p (k n)")),
                         reads=deps, writes=[b_slab], dma=True)

                def phaseC_gen(ti, t0, tw):
                    for g4 in range(4):
                        P.op("sync", lambda e, g4=g4: e.dma_start(
                            out=ht[:, 4 * g4:4 * g4 + 4, :tw], in_=hsrc_r[:, 4 * g4:4 * g4 + 4, t0:t0 + tw]),
                             reads=[b_h[ti][g4]], writes=b_ht[4 * g4:4 * g4 + 4], dma=True)
                    for g4 in range(4):
                        P.op("sync", lambda e, g4=g4: e.dma_start(
                            out=mx[:, 4 * g4:4 * g4 + 4, :tw], in_=mx_r[:, 4 * g4:4 * g4 + 4, t0:t0 + tw]),
                             reads=[b_mx[c][ti] for c in range(4 * g4, 4 * g4 + 4)], writes=[b_mxs[g4]], dma=True)
                    for j in range(4):
                        (slab, b_slab) = slabs.next()
                        load_slab_c(slab, b_slab, 512, wb_out[l][j], wdeps(f"out{l}", 0, D))
                        for i in range(4):
                            oc = 4 * j + i
                            (mb, b_mb) = mbanks.next()
                            for k in range(KC):
                                P.op("tensor", lambda e, mb=mb, slab=slab, i=i, k=k: e.matmul(
                                    mb[:, :tw], lhsT=slab[:, 512 * k + 128 * i:512 * k + 128 * i + 128],
                                    rhs=mx[:, k, :tw], start=(k == 0), stop=(k == KC - 1), skip_group_check=True),
                                     reads=[b_slab, b_mxs[k // 4]], writes=[b_mb])
                            P.op("vector", lambda e, mb=mb, oc=oc: e.tensor_tensor(
                                out=ht[:, oc, :tw], in0=mb[:, :tw], in1=ht[:, oc, :tw], op=ALU.add),
                                 reads=[b_mb, b_ht[oc]], writes=[b_ht[oc]])
                            (sq, b_sq) = sqs.next()
                            P.op("scalar", lambda e, sq=sq, oc=oc: e.activation(
                                out=sq[:, :tw], in_=ht[:, oc, :tw], func=AF.Square),
                                 reads=[b_ht[oc]], writes=[b_sq])
                            yield
                            P.op("tensor", lambda e, sq=sq, oc=oc: e.matmul(
                                stbk[0][:, :tw], lhsT=ones[:], rhs=sq[:, :tw], start=(oc == 0), stop=(oc == KC - 1),
                                skip_group_check=True),
                                 reads=[b_sq, b_const], writes=[stbk[1]])
                    rstd_from(P, stbk[0], stbk[1], ln0, rs0, 1.0 / D, tw)
                    for c in range(KC):
                        P.op("vector", lambda e, c=c: e.scalar_tensor_tensor(
                            out=hn2[:, c, :tw], in0=ht[:, c, :tw], scalar=ppc(l, O_GMLP + c),
                            in1=rs0[0][:, :tw], op0=ALU.mult, op1=ALU.mult),
                             reads=[b_ht[c], rs0[1], b_pp], writes=[b_mxs[c // 4]], nosame=True)
                    for half in range(2):
                        for j in range(8 * half, 8 * half + 8):
                            (slab, b_slab) = slabs.next()
                            load_slab_c(slab, b_slab, 512, wb_1[l][j], wdeps(f"w1{l}", 0, D))
                            for i in range(4):
                                f = 4 * j + i - 32 * half
                                (mb, b_mb) = mbanks.next()
                                for k in range(KC):
                                    P.op("tensor", lambda e, mb=mb, slab=slab, i=i, k=k: e.matmul(
                                        mb[:, :tw], lhsT=slab[:, 512 * k + 128 * i:512 * k + 128 * i + 128],
                                        rhs=hn2[:, k, :tw], start=(k == 0), stop=(k == KC - 1), skip_group_check=True),
                                         reads=[b_slab, b_mxs[k // 4]], writes=[b_mb])
                                (rl, b_rl) = rls.next()
                                P.op("scalar", lambda e, mb=mb, rl=rl: e.activation(
                                    out=rl[:, :tw], in_=mb[:, :tw], func=AF.Relu),
                                     reads=[b_mb], writes=[b_rl])
                                P.op("vector", lambda e, mb=mb, rl=rl, f=f: e.tensor_tensor(
                                    out=hid[:, f, :tw], in0=mb[:, :tw], in1=rl[:, :tw], op=ALU.mult),
                                     reads=[b_mb, b_rl], writes=[b_hid[f]])
                                yield
                        for j2 in range(8):
                            (slab, b_slab) = slabs.next()
                            load_slab_c(slab, b_slab, 256, wb_2[l][half, j2],
                                        wdeps(f"w2{l}", 4096 * half, 4096 * half + 4096))
                            for i in range(2):
                                oc = 2 * j2 + i
                                (mb, b_mb) = mbanks.next()
                                for k in range(32):
                                    P.op("tensor", lambda e, mb=mb, slab=slab, i=i, k=k: e.matmul(
                                        mb[:, :tw], lhsT=slab[:, 256 * k + 128 * i:256 * k + 128 * i + 128],
                                        rhs=hid[:, k, :tw], start=(k == 0), stop=(k == 31), skip_group_check=True),
                                         reads=[b_slab, b_hid[k]], writes=[b_mb])
                                    if k == 15:
                                        yield
                                P.op("vector", lambda e, mb=mb, oc=oc: e.tensor_tensor(
                                    out=ht[:, oc, :tw], in0=mb[:, :tw], in1=ht[:, oc, :tw], op=ALU.add),
                                     reads=[b_mb, b_ht[oc]], writes=[b_ht[oc]])
                                yield
                            if half == 1 and j2 % 2 == 1:
                                j = j2 // 2
                                if last:
                                    P.op("gpsimd", lambda e, j=j: e.dma_start(
                                        out=o_r[:, 4 * j:4 * j + 4, t0 - 128:t0 - 128 + tw],
                                        in_=ht[:, 4 * j:4 * j + 4, :tw]),
                                         reads=b_ht[4 * j:4 * j + 4], dma=True)
                                else:
                                    P.op("gpsimd", lambda e, j=j: e.dma_start(
                                        out=hT_r[:, 4 * j:4 * j + 4, t0:t0 + tw], in_=ht[:, 4 * j:4 * j + 4, :tw]),
                                         reads=b_ht[4 * j:4 * j + 4], writes=[b_h[ti][j]], dma=True)

                def step(g):
                    try:
                        next(g)
                        return True
                    except StopIteration:
                        return False

                def b_streams(ti):
                    (q0, qw) = TILES[ti]
                    return [(h, ti, q0, qw) for h in range(8)]

                def run_merged(cgen, bspecs):
                    pend = list(bspecs)
                    act = [None, None]
                    c_alive = cgen is not None
                    while c_alive or pend or act[0] is not None or act[1] is not None:
                        for s in range(2):
                            if act[s] is None and pend:
                                (h, ti, q0, qw) = pend.pop(0)
                                act[s] = AT.stream(h, ti, q0, qw, s)
                            if act[s] is not None:
                                if not step(act[s]):
                                    act[s] = None
                        if c_alive:
                            c_alive = step(cgen)

                run_merged(None, b_streams(0))
                for ti, (t0, tw) in enumerate(TILES):
                    cgen = None if (last and ti == 0) else phaseC_gen(ti, t0, tw)
                    bs = b_streams(ti + 1) if ti + 1 < NT else []
                    run_merged(cgen, bs)
                    if l == 0:
                        emit_conv("in1", 2)
                        emit_conv("rest1", 11)
                if l == 0:
                    emit_conv("in1")
                    emit_conv("rest1")
                P.barrier()
                P.flush()
    return nc


_NC_CACHE = {}


def _prep_inputs(inputs, NT=9):
    S = 128 + 512 * (NT - 1)
    nreal = S - 128
    x = np.asarray(inputs["x"], dtype=np.float32)
    meta = np.asarray(inputs["meta_tokens"], dtype=np.float32)
    B = x.shape[0]
    pp = np.zeros((128, DEPTH * NPP), np.float32)
    for l in range(DEPTH):
        o = l * NPP
        pp[:, o + 0:o + 16] = np.asarray(inputs["g_mix"][l]).reshape(16, 128).T
        pp[:, o + 16:o + 32] = np.asarray(inputs["g_mlp"][l]).reshape(16, 128).T
        cw = np.asarray(inputs["conv_w"][l])
        for k in range(3):
            pp[:, o + 32 + 8 * k:o + 32 + 8 * k + 8] = cw[k].reshape(8, 128).T
        pp[:, o + 56:o + 64] = np.asarray(inputs["g_conv_out"][l]).reshape(8, 128).T
        pp[:, o + 64:o + 72] = np.asarray(inputs["g_attn_out"][l]).reshape(8, 128).T
        pp[:, o + 72] = np.asarray(inputs["g_q"][l])
        pp[:, o + 73] = np.asarray(inputs["g_k"][l])
    shared = {
        "pp": pp,
        "w_in": np.ascontiguousarray(inputs["w_in"], dtype=np.float32),
        "w_out": np.ascontiguousarray(inputs["w_out"], dtype=np.float32),
        "w_mlp_in": np.ascontiguousarray(inputs["w_mlp_in"], dtype=np.float32),
        "w_mlp_out": np.ascontiguousarray(inputs["w_mlp_out"], dtype=np.float32),
    }
    in_maps = []
    for b in range(B):
        h0 = np.zeros((D, S), np.float32)
        h0[:, PAD:PAD + NMETA] = meta.T
        h0[:, 128:] = x[b, :nreal].T
        m = dict(shared)
        m["h0"] = h0
        in_maps.append(m)
    return in_maps


def kernel(**inputs):
    NT = 9
    if NT not in _NC_CACHE:
        _NC_CACHE[NT] = build(NT)
    nc = _NC_CACHE[NT]
    in_maps = _prep_inputs(inputs, NT)
    res = run_bass_kernel_spmd(nc, in_maps, core_ids=list(range(8)))
    out = np.stack([np.ascontiguousarray(r["outT"].T) for r in res.results], axis=0)
    return out.astype(np.float32)
```
